# Optimizing a Trainium2 kernel written in Bass

```python
import jax, jax.numpy as jnp
from jax import lax
import numpy as np

D_MODEL = 1024
BATCH = 32
SEQ = 2048
DEPTH = 1

D_RNN = 1024
N_LRU_BLOCKS = 16
LRU_BLOCK = D_RNN // N_LRU_BLOCKS
CONV_WIDTH = 4
LRU_C = 8.0
ATTN_GROUPS = ((128, 1), (512, 4), (2048, 16))
N_GROUPS = len(ATTN_GROUPS)
HEADS_PER_GROUP = 4
HEAD_DIM = 128
ATTN_WIDTH = N_GROUPS * HEADS_PER_GROUP * HEAD_DIM
ATTN_OUT_WIDTH = HEADS_PER_GROUP * HEAD_DIM
ROPE_DIM = HEAD_DIM // 4
ROPE_THETA = 500000.0
Q_BLOCK = 128
N_BRANCHES = 2
D_FF = ((8 * D_MODEL // 3 + 255) // 256) * 256
IN_WIDTH = 2 * D_RNN + 3 * ATTN_WIDTH + N_BRANCHES * D_MODEL
EPS = 1e-6
NEG = -1e30

kernel_name = "hawk_dilated_attn_hybrid_block"


def rms_norm(x, g):
    xf = x.astype(jnp.float32)
    y = xf * lax.rsqrt(jnp.mean(xf * xf, axis=-1, keepdims=True) + EPS)
    return (y * g.astype(jnp.float32)).astype(x.dtype)


def causal_depthwise_conv(x, w, b):
    y = lax.conv_general_dilated(
        x, w[:, None, :].astype(x.dtype), window_strides=(1,),
        padding=((CONV_WIDTH - 1, 0),), dimension_numbers=('NWC', 'WIO', 'NWC'),
        feature_group_count=x.shape[-1])
    return y + b.astype(x.dtype)


def rg_lru(x, w_rg, b_rg, w_ig, b_ig, lam):
    B, S, _ = x.shape
    xf = x.astype(jnp.float32)
    xb = xf.reshape(B, S, N_LRU_BLOCKS, LRU_BLOCK)
    r = jax.nn.sigmoid(jnp.einsum('bsnc,ncd->bsnd', xb, w_rg.astype(jnp.float32)).reshape(B, S, D_RNN) + b_rg.astype(jnp.float32))
    i = jax.nn.sigmoid(jnp.einsum('bsnc,ncd->bsnd', xb, w_ig.astype(jnp.float32)).reshape(B, S, D_RNN) + b_ig.astype(jnp.float32))
    log_a = -LRU_C * r * jax.nn.softplus(-lam.astype(jnp.float32))
    a = jnp.exp(log_a)
    mult = jnp.sqrt(-jnp.expm1(2.0 * log_a))
    mult = jnp.where(jnp.arange(S)[None, :, None] == 0, 1.0, mult)
    u = mult * (i * xf)

    def combine(left, right):
        a1, b1 = left
        a2, b2 = right
        return a1 * a2, a2 * b1 + b2

    _, h = lax.associative_scan(combine, (a, u), axis=1)
    return h.astype(x.dtype)


def partial_rope(t, cos, sin):
    half = ROPE_DIM // 2
    t1 = t[..., :half]
    t2 = t[..., half:ROPE_DIM]
    return jnp.concatenate([t1 * cos - t2 * sin, t2 * cos + t1 * sin, t[..., ROPE_DIM:]], axis=-1)


def dilated_window_attention(q, k, v, window, dilation):
    B, S, H, Dh = q.shape
    w_sub = window // dilation
    L = -(-S // dilation)
    nb = -(-L // Q_BLOCK)
    Lp = nb * Q_BLOCK
    pad = Lp * dilation - S

    def to_blocks(t):
        t = jnp.pad(t.astype(jnp.float32), ((0, 0), (0, pad), (0, 0), (0, 0)))
        t = t.reshape(B, Lp, dilation, H, Dh).transpose(0, 2, 1, 3, 4)
        return t.reshape(B, dilation, nb, Q_BLOCK, H, Dh)

    def with_prev(t):
        prev = jnp.pad(t[:, :, :-1], ((0, 0), (0, 0), (1, 0), (0, 0), (0, 0), (0, 0)))
        return jnp.concatenate([prev, t], axis=3)

    qb = to_blocks(q)
    kb = with_prev(to_blocks(k))
    vb = with_prev(to_blocks(v))

    qi = jnp.arange(Q_BLOCK)[:, None]
    kj = jnp.arange(2 * Q_BLOCK)[None, :]
    diff = qi + Q_BLOCK - kj
    band = (diff >= 0) & (diff <= w_sub)
    has_prev = (jnp.arange(nb)[:, None, None] > 0) | (kj[None] >= Q_BLOCK)
    mask = band[None] & has_prev

    s = jnp.einsum('brnqhc,brnkhc->brnhqk', qb, kb)
    s = jnp.where(mask[None, None, :, None], s, NEG)
    m = jnp.max(s, axis=-1, keepdims=True)
    p = jnp.exp(s - m)
    den = jnp.sum(p, axis=-1, keepdims=True)
    o = jnp.einsum('brnhqk,brnkhc->brnqhc', p, vb) / jnp.swapaxes(den, 3, 4)
    lse = jnp.swapaxes((m + jnp.log(den))[..., 0], 3, 4)

    o = o.reshape(B, dilation, Lp, H, Dh).transpose(0, 2, 1, 3, 4).reshape(B, Lp * dilation, H, Dh)[:, :S]
    lse = lse.reshape(B, dilation, Lp, H).transpose(0, 2, 1, 3).reshape(B, Lp * dilation, H)[:, :S]
    return o, lse


def setup_inputs(seed: int = 0) -> dict:
    key = jax.random.key(seed)
    ks = jax.random.split(key, 24)
    f32 = jnp.float32

    def nrm(k, shape, fan_in):
        return jax.random.normal(k, shape, f32) * (fan_in ** -0.5)

    def gain(k):
        return 1.0 + 0.02 * jax.random.normal(k, (DEPTH, D_MODEL), f32)

    x = jax.random.normal(ks[0], (BATCH, SEQ, D_MODEL), f32)
    offset = jax.random.randint(ks[1], (BATCH, 1), 0, 4096, dtype=jnp.int32)
    positions = offset + jnp.arange(SEQ, dtype=jnp.int32)[None, :]
    u = jax.random.uniform(ks[2], (DEPTH, D_RNN), f32, 0.9, 0.999)
    a_base = u ** (1.0 / LRU_C)
    lru_lambda = jnp.log(a_base) - jnp.log1p(-a_base)
    return {
        "x": x,
        "positions": positions,
        "pre_mix_norm": gain(ks[3]),
        "w_in": nrm(ks[4], (DEPTH, D_MODEL, IN_WIDTH), D_MODEL),
        "conv_w": nrm(ks[5], (DEPTH, CONV_WIDTH, D_RNN), CONV_WIDTH),
        "conv_b": 0.01 * jax.random.normal(ks[6], (DEPTH, D_RNN), f32),
        "w_rg": nrm(ks[7], (DEPTH, N_LRU_BLOCKS, LRU_BLOCK, LRU_BLOCK), LRU_BLOCK),
        "b_rg": 0.01 * jax.random.normal(ks[8], (DEPTH, D_RNN), f32),
        "w_ig": nrm(ks[9], (DEPTH, N_LRU_BLOCKS, LRU_BLOCK, LRU_BLOCK), LRU_BLOCK),
        "b_ig": 0.01 * jax.random.normal(ks[10], (DEPTH, D_RNN), f32),
        "lru_lambda": lru_lambda,
        "w_lru_proj": nrm(ks[11], (DEPTH, D_RNN, D_MODEL), D_RNN),
        "w_attn_proj": nrm(ks[12], (DEPTH, ATTN_OUT_WIDTH, D_MODEL), ATTN_OUT_WIDTH),
        "w_out": nrm(ks[13], (DEPTH, D_MODEL, D_MODEL), D_MODEL),
        "post_mix_norm": gain(ks[14]),
        "pre_ffn_norm": gain(ks[15]),
        "w_ffn_gate": nrm(ks[16], (DEPTH, D_MODEL, D_FF), D_MODEL),
        "w_ffn_up": nrm(ks[17], (DEPTH, D_MODEL, D_FF), D_MODEL),
        "w_ffn_down": nrm(ks[18], (DEPTH, D_FF, D_MODEL), D_FF),
        "post_ffn_norm": gain(ks[19]),
    }


def reference(x, positions, pre_mix_norm, w_in, conv_w, conv_b, w_rg, b_rg, w_ig, b_ig,
              lru_lambda, w_lru_proj, w_attn_proj, w_out, post_mix_norm, pre_ffn_norm,
              w_ffn_gate, w_ffn_up, w_ffn_down, post_ffn_norm):
    B, S, _ = x.shape
    dt = x.dtype
    inv_freq = ROPE_THETA ** (-jnp.arange(0, ROPE_DIM, 2, dtype=jnp.float32) / ROPE_DIM)
    ang = positions.astype(jnp.float32)[..., None] * inv_freq
    cos = jnp.cos(ang)[:, :, None, :].astype(dt)
    sin = jnp.sin(ang)[:, :, None, :].astype(dt)
    split_at = np.cumsum([D_RNN, D_RNN, ATTN_WIDTH, ATTN_WIDTH, ATTN_WIDTH]).tolist()

    for l in range(DEPTH):
        h = rms_norm(x, pre_mix_norm[l])
        proj = h @ w_in[l].astype(dt)
        xr, gr, q, k, v, gates = jnp.split(proj, split_at, axis=-1)

        xr = causal_depthwise_conv(xr, conv_w[l], conv_b[l])
        xr = rg_lru(xr, w_rg[l], b_rg[l], w_ig[l], b_ig[l], lru_lambda[l])
        y_lru = (xr * jax.nn.gelu(gr)) @ w_lru_proj[l].astype(dt)

        q = q.reshape(B, S, N_GROUPS, HEADS_PER_GROUP, HEAD_DIM)
        k = k.reshape(B, S, N_GROUPS, HEADS_PER_GROUP, HEAD_DIM)
        v = v.reshape(B, S, N_GROUPS, HEADS_PER_GROUP, HEAD_DIM)
        q = partial_rope(q, cos[:, :, None], sin[:, :, None]) * (HEAD_DIM ** -0.5)
        k = partial_rope(k, cos[:, :, None], sin[:, :, None])
        outs, lses = [], []
        for g, (window, dilation) in enumerate(ATTN_GROUPS):
            o_g, lse_g = dilated_window_attention(q[:, :, g], k[:, :, g], v[:, :, g], window, dilation)
            outs.append(o_g)
            lses.append(lse_g)
        o = jnp.stack(outs, axis=2)
        wts = jax.nn.softmax(jnp.stack(lses, axis=2), axis=2)
        o = jnp.sum(wts[..., None] * o, axis=2).reshape(B, S, ATTN_OUT_WIDTH).astype(dt)
        y_attn = o @ w_attn_proj[l].astype(dt)

        g_lru, g_attn = jnp.split(jax.nn.sigmoid(gates), N_BRANCHES, axis=-1)
        mix = (g_lru * y_lru + g_attn * y_attn) @ w_out[l].astype(dt)
        x = x + rms_norm(mix, post_mix_norm[l])

        h = rms_norm(x, pre_ffn_norm[l])
        f = (jax.nn.silu(h @ w_ffn_gate[l].astype(dt)) * (h @ w_ffn_up[l].astype(dt))) @ w_ffn_down[l].astype(dt)
        x = x + rms_norm(f, post_ffn_norm[l])
    return x
```

```python
import math
from contextlib import ExitStack

import numpy as np
import concourse.bass as bass
import concourse.mybir as mybir
from concourse.bass_utils import run_bass_kernel_spmd

F32 = mybir.dt.float32
BF16 = mybir.dt.bfloat16
I32 = mybir.dt.int32
ALU = mybir.AluOpType
AF = mybir.ActivationFunctionType

ENGS = ("pe", "act", "dve", "pool", "sp")
NSLOT = 8

D = 1024
SEQ = 2048
NCORES = 8
D_FF = 2816
NFC = D_FF // 128
EPS = 1e-6
GROUP_DIL = (1, 4, 16)
LRU_COLS = 2048
GRP_COLS = 1792
GATE_OFF = LRU_COLS + 3 * GRP_COLS
NCOLS = GATE_OFF + 2048
OAW = 516


class Sched:
    def __init__(self, nc, same_eng_sync=True):
        self.nc = nc
        self.same_eng_sync = same_eng_sync
        self.lists = {e: [] for e in ENGS}
        self.count = {e: 0 for e in ENGS}
        self.waited = {e: {} for e in ENGS}
        self.last_w = {}
        self.readers = {}
        self.slot_uses = {}
        self.slot_next = {e: 0 for e in ENGS}
        self.sems = {}
        self.sem_names = [f"s_{e}" for e in ENGS if e != "sp"]

    def _deps(self, reads, writes):
        deps = {}
        for k in reads:
            t = self.last_w.get(k)
            if t is not None:
                deps[t[0]] = max(deps.get(t[0], 0), t[1])
        for k in writes:
            t = self.last_w.get(k)
            if t is not None:
                deps[t[0]] = max(deps.get(t[0], 0), t[1])
            for t in self.readers.get(k, ()):
                deps[t[0]] = max(deps.get(t[0], 0), t[1])
        return deps

    def _emit_waits(self, eng, deps):
        own = f"s_{eng}"
        for sem, val in deps.items():
            if sem == own and (eng == "pe" or not self.same_eng_sync):
                continue
            if self.waited[eng].get(sem, 0) >= val:
                continue
            self.waited[eng][sem] = val
            self.lists[eng].append(("wait", sem, val))

    def _record(self, token, reads, writes):
        for k in writes:
            self.last_w[k] = token
            self.readers[k] = []
        for k in reads:
            self.readers.setdefault(k, []).append(token)

    def op(self, eng, fn, reads=(), writes=()):
        self._emit_waits(eng, self._deps(reads, writes))
        self.count[eng] += 1
        token = (f"s_{eng}", self.count[eng])
        self.lists[eng].append(("op", fn, f"s_{eng}", 1))
        self._record(token, reads, writes)

    def dma(self, q, fn, reads=(), writes=()):
        slot = self.slot_next[q]
        self.slot_next[q] = (slot + 1) % NSLOT
        sem = f"d_{q}_{slot}"
        uses = self.slot_uses.get(sem, 0)
        deps = self._deps(reads, writes)
        if uses > 0:
            deps[sem] = max(deps.get(sem, 0), 16 * uses)
        self._emit_waits(q, deps)
        self.slot_uses[sem] = uses + 1
        token = (sem, 16 * (uses + 1))
        self.lists[q].append(("op", fn, sem, 16))
        self._record(token, reads, writes)

    def finish(self):
        for sem, uses in self.slot_uses.items():
            q = sem.split("_")[1]
            self._emit_waits(q, {sem: 16 * uses})

    def emit(self):
        nc = self.nc
        names = list(self.sem_names) + sorted(self.slot_uses.keys())
        with ExitStack() as st:
            for n in names:
                self.sems[n] = st.enter_context(nc.semaphore(n))
            block = st.enter_context(nc.Block())

            def replay(ename):
                def body(e):
                    for item in self.lists[ename]:
                        if item[0] == "wait":
                            e.wait_ge(self.sems[item[1]], item[2])
                        else:
                            item[1](e).then_inc(self.sems[item[2]], item[3])
                return body

            for ename, reg in (("sp", block.sync), ("pe", block.tensor), ("act", block.scalar),
                               ("dve", block.vector), ("pool", block.gpsimd)):
                if self.lists[ename]:
                    reg(replay(ename))


def build(nseq, stop=None):
    nc = bass.Bass("TRN2", target_bir_lowering=False)
    dt_in = lambda n, s, d=F32: nc.dram_tensor(n, s, d, kind="ExternalInput").ap()
    x_d = dt_in("x", [nseq, SEQ, D])
    pos_d = dt_in("pos_bc", [nseq, 128, SEQ], I32)
    gains_d = dt_in("gains_bc", [4, 128, D])
    win_d = dt_in("w_in_t", [128, 8, NCOLS])
    fvec_d = dt_in("fvec", [128, 64])
    ropec_d = dt_in("rope_c", [128, 2])
    wrg_d = dt_in("wrg_blk", [128, 8, 128])
    wig_d = dt_in("wig_blk", [128, 8, 128])
    wlru_d = dt_in("w_lru_t", [128, 8, D])
    wattn_d = dt_in("w_attn_t", [128, 4, D])
    wout_d = dt_in("w_out_t", [128, 8, D])
    wgu_d = dt_in("w_gu_t", [128, 8, 2 * D_FF])
    wd_d = dt_in("w_down_t", [128, NFC, D])
    y_d = nc.dram_tensor("y", [nseq, SEQ, D], F32, kind="ExternalOutput").ap()
    oa_d = nc.dram_tensor("oa_scr", [3, SEQ, OAW], F32, kind="ExternalOutput").ap()
    yl_d = nc.dram_tensor("yl_scr", [D, SEQ], BF16, kind="ExternalOutput").ap()
    x1_d = nc.dram_tensor("x1_scr", [SEQ, D], F32, kind="ExternalOutput").ap()

    wsrc = {"win": win_d, "wlru": wlru_d, "wattn": wattn_d, "wout": wout_d, "wgu": wgu_d, "wd": wd_d}
    wbf = {k: nc.dram_tensor(k + "_bf", list(v.shape), BF16, kind="ExternalOutput").ap() for k, v in wsrc.items()}

    with ExitStack() as st:
        sb = lambda n, s, d: st.enter_context(nc.sbuf_tensor(n, s, d))
        S = Sched(nc)
        gbc = sb("gbc", [128, 4, D], F32)
        C4 = sb("C4", [128, SEQ], F32)
        S4 = sb("S4", [128, SEQ], F32)
        hT = sb("hT", [128, 8, SEQ], BF16)
        ylw = sb("ylw", [128, 8, 512], BF16)
        arena = sb("arena", [128, 24832], BF16)
        wsl = [sb(f"wsl{i}", [128, 11, 512], BF16) for i in range(2)]
        wsw = sb("wsw", [128, 8, 128], BF16)
        wrg = sb("wrg", [128, 8, 128], BF16)
        wig = sb("wig", [128, 8, 128], BF16)
        xt = [sb(f"xt{i}", [128, D], F32) for i in range(2)]
        NT = 8
        T = [sb(f"T{i}", [128, 512], F32) for i in range(NT)]
        xre = [sb(f"xre{i}", [128, 515], F32) for i in range(2)]
        hb = sb("hb", [128, D], BF16)
        junk = hb
        fbuf = sb("fbuf", [128, 4, 512], F32)
        Pc = sb("Pc", [128, 512], BF16)
        Pp = sb("Pp", [128, 512], BF16)
        xcb = sb("xcb", [128, 512], BF16)
        ob = xcb
        xcb2 = sb("xcb2", [128, 512], BF16)
        oe = sb("oe", [128, OAW], F32)
        oa3_a = sb("oa3", [128, 3, OAW], F32)
        x1t = sb("x1t", [128, D], F32)
        posi = sb("posi", [128, 512], I32)
        ident = sb("ident", [128, 128], BF16)
        mcur = sb("mcur", [128, 4, 128], BF16)
        mprev = sb("mprev", [128, 4, 128], BF16)
        ones1 = sb("ones1", [128, 1], BF16)
        fvec = sb("fvec_sb", [128, 64], F32)
        fder = sb("fder", [128, 40], F32)
        ropec = sb("ropec", [128, 2], F32)
        sm = sb("sm", [128, 32], F32)
        hprev = sb("hprev", [128, 1], F32)
        halos = sb("halos", [128, 4, 3], F32)
        qT = arena[:, 0:8192].rearrange("p (h t) -> p h t", h=4)
        kT = arena[:, 8192:16384].rearrange("p (h t) -> p h t", h=4)
        V = arena[:, 16384:24576].rearrange("p (b c) -> p b c", b=16)
        OT = arena[:, 0:2048].rearrange("p (h t) -> p h t", h=4)
        mergedT = arena[:, 2048:6144].rearrange("p (k t) -> p k t", k=8)
        h2T = arena[:, 6144:10240].rearrange("p (k t) -> p k t", k=8)
        aT = arena[:, 10240:21504].rearrange("p (k t) -> p k t", k=NFC)
        PB = [st.enter_context(nc.psum_tensor(f"pb{i}", [128, 512], F32)) for i in range(7)]
        PT = st.enter_context(nc.psum_tensor("pt", [128, 1024], BF16))
        bank_ctr = [0]

        def bank():
            b = bank_ctr[0] % 6
            bank_ctr[0] += 1
            return b

        def bank_pair():
            if bank_ctr[0] % 2:
                bank_ctr[0] += 1
            b = bank_ctr[0] % 6
            bank_ctr[0] += 2
            return b, b + 1

        tctr = [0]

        def tmp():
            i = tctr[0] % NT
            tctr[0] += 1
            return T[i], ("T", i)

        def ACT(out, in_, func, r, w, scale=1.0, bias=0.0, accum=None):
            S.op("act", lambda e: e.activation(out, in_, func, bias=bias, scale=scale, accum_out=accum), r, w)

        def TS(eng, out, in0, s1, s2, op0, op1, r, w):
            if s2 is None:
                S.op(eng, lambda e: e.tensor_scalar(out, in0, s1, None, op0), r, w)
            else:
                S.op(eng, lambda e: e.tensor_scalar(out, in0, s1, s2, op0, op1), r, w)

        def TT(eng, out, in0, in1, op, r, w):
            S.op(eng, lambda e: e.tensor_tensor(out, in0, in1, op), r, w)

        def STT(out, in0, sc, in1, op0, op1, r, w):
            S.op("dve", lambda e: e.scalar_tensor_tensor(out, in0, sc, in1, op0, op1), r, w)

        def CP(eng, out, in_, r, w):
            S.op(eng, lambda e: e.tensor_copy(out, in_), r, w)

        def DMA(q, out, in_, r, w):
            S.dma(q, lambda e: e.dma_start(out=out, in_=in_), r, w)

        def MM(out, pairs, r, w, first=True, last=True):
            def fn(e):
                n = len(pairs)
                for i, (l, rr) in enumerate(pairs):
                    ins = e.matmul(out, l, rr, start=(first and i == 0), stop=(last and i == n - 1))
                return ins
            S.op("pe", fn, r, w)

        def TR(outs_ins, r, w):
            def fn(e):
                for o, i in outs_ins:
                    ins = e.transpose(o, i, ident[:])
                return ins
            S.op("pe", fn, list(r) + ["ident"], w)

        slab_ctr = [0]

        def next_slab(name, k0, nk, c0, ncol):
            i = slab_ctr[0] % 2
            slab_ctr[0] += 1
            DMA("sp", wsl[i][:, 0:nk, 0:ncol], wbf[name][:, k0:k0 + nk, c0:c0 + ncol],
                [("wbf", name, kc, 1 if (name == "win" and c0 >= LRU_COLS) else 0) for kc in range(k0, k0 + nk)], [("wsl", i)])
            return wsl[i], ("wsl", i)

        def rstd_from_ss(ss_ap, dst_ap, keys_r, key_w, post_scale=None):
            TS("dve", dst_ap, ss_ap, 1.0 / D, EPS, ALU.mult, ALU.add, keys_r, [key_w])
            ACT(dst_ap, dst_ap, AF.Sqrt, [key_w], [key_w])
            S.op("dve", lambda e: e.reciprocal(dst_ap, dst_ap), [key_w], [key_w])
            if post_scale is not None:
                TS("dve", dst_ap, dst_ap, post_scale, None, ALU.mult, None, [key_w], [key_w])

        DMA("sp", gbc[:], gains_d.rearrange("g p d -> p g d"), [], ["gbc"])
        DMA("sp", fvec[:], fvec_d, [], ["fvec"])
        DMA("sp", ropec[:], ropec_d, [], ["ropec"])
        DMA("pool", wrg[:], wrg_d, [], ["wrg"])
        DMA("pool", wig[:], wig_d, [], ["wig"])
        S.op("pool", lambda e: e.memset(ident[:], 0.0), [], ["ident"])
        S.op("pool", lambda e: e.affine_select(ident[:], ident[:], [[1, 128]], ALU.not_equal, 1.0,
                                               base=0, channel_multiplier=-1), ["ident"], ["ident"])
        S.op("pool", lambda e: e.memset(mcur[:], 1.0), [], ["mcur"])
        S.op("pool", lambda e: e.affine_select(mcur[:], mcur[:], [[0, 4], [1, 128]], ALU.is_ge, 0.0,
                                               base=0, channel_multiplier=-1), ["mcur"], ["mcur"])
        S.op("pool", lambda e: e.memset(mprev[:], 1.0), [], ["mprev"])
        S.op("pool", lambda e: e.affine_select(mprev[:], mprev[:], [[0, 4], [-1, 128]], ALU.is_ge, 0.0,
                                               base=0, channel_multiplier=1), ["mprev"], ["mprev"])
        S.op("pool", lambda e: e.memset(ones1[:], 1.0), [], ["ones1"])
        TS("pool", mcur[:], mcur[:], 30000.0, -30000.0, ALU.mult, ALU.add, ["mcur"], ["mcur"])
        TS("pool", mprev[:], mprev[:], 30000.0, -30000.0, ALU.mult, ALU.add, ["mprev"], ["mprev"])
        TS("dve", fder[:, 0:16], fvec[:, 40:56], 0.5, None, ALU.mult, None, ["fvec"], ["fder"])
        ACT(fder[:, 32:40], fvec[:, 56:64], AF.Exp, ["fvec", "fder"], ["fder"], scale=-1.0)
        ACT(fder[:, 32:40], fder[:, 32:40], AF.Ln, ["fder"], ["fder"], bias=1.0)
        TS("dve", fder[:, 16:24], fder[:, 32:40], -8.0, None, ALU.mult, None, ["fder"], ["fder"])
        TS("dve", fder[:, 24:32], fder[:, 32:40], -4.0, None, ALU.mult, None, ["fder"], ["fder"])
        cw = lambda j, c: fvec[:, j * 8 + c:j * 8 + c + 1]
        cb = lambda c: fvec[:, 32 + c:33 + c]
        hbrg = lambda c: fder[:, c:c + 1]
        hbig = lambda c: fder[:, 8 + c:9 + c]
        c1v = lambda c: fder[:, 16 + c:17 + c]
        hc1 = lambda c: fder[:, 24 + c:25 + c]
        QS = 1.0 / math.sqrt(128.0)
        hTk = lambda w: ("hT", w)
        ALLHT = [hTk(w) for w in range(4)]
        CKEYS = [("qT", h) for h in range(4)] + [("kT", h) for h in range(4)] + [("V", bl) for bl in range(16)]
        EKEYS = ["OT", "h2T"] + [("mg", m) for m in range(8)] + [("aT", c) for c in range(NFC)]

        for kc in range(8):
            S.dma("pool", lambda e, o=wbf["win"][:, kc, 0:LRU_COLS], i=win_d[:, kc, 0:LRU_COLS]: e.dma_start(out=o, in_=i),
                  [], [("wbf", "win", kc, 0)])
        for kc in range(8):
            S.dma("pool", lambda e, o=wbf["win"][:, kc, LRU_COLS:NCOLS], i=win_d[:, kc, LRU_COLS:NCOLS]: e.dma_start(out=o, in_=i),
                  [], [("wbf", "win", kc, 1)])
        for k_, v_ in wsrc.items():
            if k_ == "win":
                continue
            for kc in range(v_.shape[1]):
                S.dma("pool", lambda e, o=wbf[k_][:, kc, :], i=v_[:, kc, :]: e.dma_start(out=o, in_=i), [], [("wbf", k_, kc, 0)])

        for b in range(nseq):
            for w in range(4):
                ws = slice(w * 512, (w + 1) * 512)
                DMA("sp", posi[:], pos_d[b, :, ws], [], ["posi"])
                ang, ka = tmp()
                CP("dve", ang[:], posi[:], ["posi"], [ka])
                TS("dve", ang[:], ang[:], ropec[:, 0:1], None, ALU.mult, None, [ka, "ropec"], [ka])
                TS("dve", posi[:], ang[:], 1.0 / (2 * math.pi), None, ALU.mult, None, [ka], ["posi"])
                kf, kk = tmp()
                CP("dve", kf[:], posi[:], ["posi"], [kk])
                STT(ang[:], kf[:], -2 * math.pi, ang[:], ALU.mult, ALU.add, [kk, ka], [ka])
                sh, ksh = tmp()
                ch, kch = tmp()
                ACT(sh[:], ang[:], AF.Sin, [ka], [ksh], scale=0.5)
                ACT(ch[:], ang[:], AF.Sin, [ka], [kch], scale=-0.5, bias=math.pi / 2)
                TT("dve", ch[:], sh[:], ch[:], ALU.mult, [ksh, kch], [kch])
                TS("dve", S4[:, ws], ch[:], ropec[:, 1:2], None, ALU.mult, None, [kch, "ropec"], [("S4", w)])
                TT("dve", sh[:], sh[:], sh[:], ALU.mult, [ksh], [ksh])
                TS("dve", C4[:, ws], sh[:], -2.0, 1.0, ALU.mult, ALU.add, [ksh], [("C4", w)])

            if stop == 'R':
                break
            for t in range(16):
                xb_ = xt[t % 2]
                kx = ("xt", t % 2)
                DMA("sp", xb_[:], x_d[b, t * 128:(t + 1) * 128, :], [], [kx])
                ACT(junk[:], xb_[:], AF.Square, [kx], ["hb", "sm0"], accum=sm[:, 0:1])
                rstd_from_ss(sm[:, 0:1], sm[:, 1:2], ["sm0"], "sm1")
                STT(hb[:], xb_[:], sm[:, 1:2], gbc[:, 0, :], ALU.mult, ALU.mult, [kx, "sm1", "gbc"], ["hb"])
                TR([(PT[:, k * 128:(k + 1) * 128], hb[:, k * 128:(k + 1) * 128]) for k in range(8)], ["hb"], ["PT"])
                ACT(hT[:, :, t * 128:(t + 1) * 128], PT[:, :].rearrange("p (k t) -> p k t", k=8), AF.Copy,
                    ["PT"], [hTk(t // 4)])

            if stop == 'A':
                break
            items = [(j, w) for j in range(8) for w in range(4)]
            bslabs = {}
            bufsets = [
                [(T[k][:], ("T", k)) for k in range(8)],
                [(fbuf[:, k, :], ("fbuf", k)) for k in range(4)] + [(x1t[:, 0:512], "x1t"), (x1t[:, 512:1024], "x1t"),
                                                                      (xt[0][:, 0:512], ("xt", 0)), (xt[0][:, 512:1024], ("xt", 0))],
            ]
            bbank = [0]

            def bbk():
                bbank[0] += 1
                return bbank[0] % 7

            def chain(i, slot):
                j, w = items[i]
                s, jj = j // 2, j % 2
                if jj == 0 and w == 0:
                    bslabs[s] = next_slab("win", 0, 8, s * 512, 512)
                slab, kslab = bslabs[s]
                ws = slice(w * 512, (w + 1) * 512)
                (gr, kgr), (xc, kxc), (tr_, ktr), (ti_, kti), (a_, kaa), (a2, ka2), (hh, khh), (p2, kp2) = bufsets[slot]
                bx, bg = bbk(), bbk()
                MM(PB[bx][:], [(slab[:, k, (2 * jj) * 128:(2 * jj + 1) * 128], hT[:, k, ws]) for k in range(8)],
                   [kslab, hTk(w)], [("pb", bx)])
                MM(PB[bg][:], [(slab[:, k, (2 * jj + 1) * 128:(2 * jj + 2) * 128], hT[:, k, ws]) for k in range(8)],
                   [kslab, hTk(w)], [("pb", bg)])
                yield
                xe = xre[w % 2]
                kxe = ("xre", w % 2)
                ACT(xe[:, 3:515], PB[bx][:], AF.Copy, [("pb", bx)], [kxe])
                ACT(gr, PB[bg][:], AF.Copy, [("pb", bg)], [kgr])
                CP("dve", halos[:, w, :], xe[:, 512:515], [kxe], [("halo", w)])
                yield
                if w == 0:
                    S.op("pool", lambda e, xe=xe: e.memset(xe[:, 0:3], 0.0), [], [kxe])
                else:
                    CP("dve", xe[:, 0:3], halos[:, w - 1, :], [("halo", w - 1)], [kxe])
                TT("pool", p2, gr, gr, ALU.mult, [kgr], [kp2])
                yield
                ACT(xc, xe[:, 0:512], AF.Identity, [kxe, "fvec"], [kxc], scale=cw(0, j), bias=cb(j))
                TS("pool", p2, p2, 0.044715, 1.0, ALU.mult, ALU.add, [kp2], [kp2])
                yield
                for q in range(1, 4):
                    STT(xc, xe[:, q:q + 512], cw(q, j), xc, ALU.mult, ALU.add, [kxe, kxc, "fvec"], [kxc])
                    if q == 1:
                        TT("pool", p2, p2, gr, ALU.mult, [kp2, kgr], [kp2])
                    yield
                xb2, kxb2 = (xcb, "xcb") if slot == 0 else (xcb2, "xcb2")
                ACT(xb2[:], xc, AF.Copy, [kxc], [kxb2])
                yield
                br, bi = bbk(), bbk()
                MM(PB[br][:], [(wrg[:, j, :], xb2[:])], ["wrg", kxb2], [("pb", br)])
                MM(PB[bi][:], [(wig[:, j, :], xb2[:])], ["wig", kxb2], [("pb", bi)])
                yield
                ACT(tr_, PB[br][:], AF.Tanh, [("pb", br), "fder"], [ktr], scale=0.5, bias=hbrg(j))
                yield
                ACT(ti_, PB[bi][:], AF.Tanh, [("pb", bi), "fder"], [kti], scale=0.5, bias=hbig(j))
                yield
                ACT(a_, tr_, AF.Exp, [ktr, "fder"], [kaa], scale=hc1(j), bias=hc1(j))
                yield
                ACT(tr_, tr_, AF.Tanh, [ktr, "fder"], [ktr], scale=hc1(j), bias=hc1(j))
                yield
                ACT(p2, p2, AF.Tanh, [kp2], [kp2], scale=0.7978845608028654)
                ACT(a2, a_, AF.Square, [kaa], [ka2])
                yield
                STT(a2, a2, 1.0, tr_, ALU.add, ALU.mult, [ka2, ktr], [ka2])
                yield
                ACT(a2, a2, AF.Sqrt, [ka2], [ka2], scale=-0.25)
                if w == 0:
                    S.op("pool", lambda e, a2=a2: e.memset(a2[:, 0:1], 0.5), [ka2], [ka2])
                STT(ti_, ti_, 1.0, xc, ALU.add, ALU.mult, [kti, kxc], [kti])
                yield
                TT("dve", ti_, ti_, a2, ALU.mult, [kti, ka2], [kti])
                STT(p2, p2, 1.0, gr, ALU.add, ALU.mult, [kp2, kgr], [kp2])
                yield
                if w == 0:
                    S.op("dve", lambda e, hh=hh, a_=a_, ti_=ti_: e.tensor_tensor_scan(
                        hh, a_, ti_, 0.0, ALU.mult, ALU.add), [kaa, kti], [khh])
                else:
                    S.op("dve", lambda e, hh=hh, a_=a_, ti_=ti_: e.tensor_tensor_scan(
                        hh, a_, ti_, hprev[:, 0:1], ALU.mult, ALU.add), [kaa, kti, "hprev"], [khh])
                CP("dve", hprev[:], hh[:, 511:512], [khh], ["hprev"])
                yield
                STT(ylw[:, slot, :], p2, 0.5, hh, ALU.mult, ALU.mult, [kp2, khh], [("ylws", slot), "ylw"])
                DMA("pool", yl_d[j * 128:(j + 1) * 128, ws], ylw[:, slot, :], [("ylws", slot)], [("yl", w)])
                yield

            for i in range(0, len(items), 2):
                gens = [chain(i, 0), chain(i + 1, 1)]
                alive = [True, True]
                while any(alive):
                    for gi, gen in enumerate(gens):
                        if alive[gi]:
                            try:
                                next(gen)
                            except StopIteration:
                                alive[gi] = False

            if stop == 'B':
                break
            for g in range(3):
                d = GROUP_DIL[g]
                nb = 16 // d
                base = LRU_COLS + g * GRP_COLS
                for which in range(2):
                    DMA("sp", wsw[:], wbf["win"][:, :, base + 1536 + which * 128:base + 1664 + which * 128], [("wbf", "win", kc, 1) for kc in range(8)], ["wsw"])
                    slab, kslab = next_slab("win", 0, 8, base + which * 512, 512)
                    dstT = qT if which == 0 else kT
                    dname = "qT" if which == 0 else "kT"
                    sc = QS if which == 0 else 1.0
                    for w in range(4):
                        ws = slice(w * 512, (w + 1) * 512)
                        bs0, bs1 = bank(), bank()
                        MM(PB[bs0][0:64, :], [(wsw[:, k, 0:64], hT[:, k, ws]) for k in range(8)],
                           ["wsw", hTk(w)], [("pb", bs0)])
                        MM(PB[bs1][0:64, :], [(wsw[:, k, 64:128], hT[:, k, ws]) for k in range(8)],
                           ["wsw", hTk(w)], [("pb", bs1)])
                        n_i = 512 // d
                        for h in range(4):
                            bm = bank()
                            MM(PB[bm][:], [(slab[:, k, h * 128:(h + 1) * 128], hT[:, k, ws]) for k in range(8)],
                               [kslab, hTk(w)], [("pb", bm)])
                            kd = (dname, h)
                            dst_full = dstT[:, h, :].rearrange("p (r i) -> p i r", r=d)[:, w * n_i:(w + 1) * n_i, :]
                            dst_rope = dstT[0:32, h, :].rearrange("p (r i) -> p i r", r=d)[:, w * n_i:(w + 1) * n_i, :]
                            src_full = PB[bm][:, :].rearrange("p (i r) -> p i r", r=d)
                            stg, kst = tmp()
                            ACT(stg[:], PB[bm][:], AF.Copy, [("pb", bm)], [kst])
                            if stop != 'C1a':
                                t1, k1 = tmp()
                                t2, k2 = tmp()
                                bs = bs0 if h < 2 else bs1
                                po = 32 * (h % 2)
                                TT("dve", t1[0:32, :], PB[bs][po:po + 32, :], S4[po:po + 32, ws], ALU.mult,
                                   [("pb", bs), ("S4", w)], [k1])
                                TT("dve", t2[0:32, :], stg[0:32, :], C4[0:32, ws], ALU.mult, [kst, ("C4", w)], [k2])
                                TT("dve", stg[0:32, :], t1[0:32, :], t2[0:32, :], ALU.add, [k1, k2, kst], [kst])
                            ACT(dstT[:, h, ws], stg[:], AF.Copy, [kst], [kd] + EKEYS, scale=sc)
                if stop in ('C1', 'C1a'):
                    break
                slab, kslab = next_slab("win", 0, 8, base + 1024, 512)
                for blk in range(16):
                    r_, n_ = blk // nb, blk % nb
                    st0 = r_ + d * 128 * n_
                    bv = bank()
                    MM(PB[bv][:], [(hT[:, k, st0:st0 + 127 * d + 1:d], slab[:, k, 0:512]) for k in range(8)],
                       [kslab] + ALLHT, [("pb", bv)])
                    ACT(V[:, blk, :], PB[bv][:], AF.Copy, [("pb", bv)], [("V", blk)] + EKEYS)
                if stop == 'C2':
                    break
                qk_r = [("qT", h) for h in range(4)] + [("kT", h) for h in range(4)]

                def emit_S(blk):
                    r_, n_ = blk // nb, blk % nb
                    has_prev = n_ > 0
                    st0 = r_ + d * 128 * n_
                    cs = slice(st0, st0 + 127 * d + 1, d)
                    ps_ = slice(st0 - 128 * d, st0 - d + 1, d)
                    outs = []
                    for ks, negm, kneg in ((cs, mcur, "mcur"),) + (((ps_, mprev, "mprev"),) if has_prev else ()):
                        bk = bank()

                        def s_fn(e, bk=bk, cs=cs, ks=ks, negm=negm):
                            ins = e.matmul(PB[bk][:], ident[:], negm[:].rearrange("p h q -> p (h q)"), start=True, stop=False)
                            for h in range(4):
                                ins = e.matmul(PB[bk][:, h * 128:(h + 1) * 128], kT[:, h, ks], qT[:, h, cs],
                                               start=False, stop=(h == 3))
                            return ins
                        S.op("pe", s_fn, qk_r + ["ident", kneg], [("pb", bk)])
                        outs.append(bk)
                    return outs

                def emit_rest(blk, banks_):
                    r_, n_ = blk // nb, blk % nb
                    has_prev = n_ > 0
                    par = blk % 2
                    Pc_, kPc = ylw[:, 2 + 2 * par, :], ("ylwp", 2 * par)
                    Pp_, kPp = ylw[:, 3 + 2 * par, :], ("ylwp", 2 * par + 1)
                    ACT(Pc_, PB[banks_[0]][:], AF.Exp, [("pb", banks_[0])], [kPc, "ylw"])
                    if has_prev:
                        ACT(Pp_, PB[banks_[1]][:], AF.Exp, [("pb", banks_[1])], [kPp, "ylw"])
                    bo = bank()

                    def pv_fn(e, bo=bo, blk=blk, has_prev=has_prev, Pc_=Pc_, Pp_=Pp_):
                        for h in range(4):
                            hs = slice(h * 128, (h + 1) * 128)
                            ins = e.matmul(PB[bo][:, hs], Pc_[:, hs], V[:, blk, hs], start=True, stop=not has_prev)
                            if has_prev:
                                ins = e.matmul(PB[bo][:, hs], Pp_[:, hs], V[:, blk - 1, hs], start=False, stop=True)
                        for h in range(4):
                            hs = slice(h * 128, (h + 1) * 128)
                            ins = e.matmul(PB[6][:, h:h + 1], Pc_[:, hs], ones1[:], start=True, stop=not has_prev)
                            if has_prev:
                                ins = e.matmul(PB[6][:, h:h + 1], Pp_[:, hs], ones1[:], start=False, stop=True)
                        return ins
                    rd = [kPc, ("V", blk), "ones1"] + ([kPp, ("V", blk - 1)] if has_prev else [])
                    S.op("pe", pv_fn, rd, [("pb", bo), ("pb", 6)])
                    oe_, koe = (oe, "oe") if par == 0 else (x1t, "x1t")
                    ACT(oe_[:, 0:512], PB[bo][:], AF.Copy, [("pb", bo)], [koe])
                    CP("dve", oe_[:, 512:516], PB[6][:, 0:4], [("pb", 6)], [koe])
                    st0 = r_ + d * 128 * n_
                    DMA("pool", oa_d[g, st0:st0 + 127 * d + 1:d, :], oe_[:, 0:OAW], [koe], [("oa", g)])

                pend = emit_S(0)
                for blk in range(16):
                    nxt = emit_S(blk + 1) if blk + 1 < 16 else None
                    emit_rest(blk, pend)
                    pend = nxt

            if stop in ('C', 'C1', 'C2', 'C1a'):
                break
            for w in range(4):
                ws = slice(w * 512, (w + 1) * 512)
                DMA("sp", ylw[:], yl_d[:, ws].rearrange("(k p) t -> p k t", p=128), [("yl", w)], [("ylws", 0), ("ylws", 1), "ylw"] + [("ylwp", k) for k in range(4)])
                for tt in range(4):
                    tk = w * 4 + tt
                    if tt % 2 == 0:
                        oa3, koa3 = oa3_a, ["oa3"]
                    else:
                        oa3 = fbuf[:].rearrange("p a b -> p (a b)")[:, 0:3 * OAW].rearrange("p (g c) -> p g c", g=3)
                        koa3 = [("fbuf", k) for k in range(4)]
                    DMA("sp", oa3[:], oa_d[:, tk * 128:(tk + 1) * 128, :].rearrange("g t c -> t g c"),
                        [("oa", 0), ("oa", 1), ("oa", 2)], koa3)
                    TT("dve", oa3[:, 0, :], oa3[:, 0, :], oa3[:, 1, :], ALU.add, koa3, koa3)
                    TT("dve", oa3[:, 0, :], oa3[:, 0, :], oa3[:, 2, :], ALU.add, koa3, koa3)
                    S.op("dve", lambda e, oa3=oa3: e.reciprocal(sm[:, 8:12], oa3[:, 0, 512:516]), koa3, ["sm8"])
                    for h in range(4):
                        TS("dve", ob[:, h * 128:(h + 1) * 128], oa3[:, 0, h * 128:(h + 1) * 128], sm[:, 8 + h:9 + h], None,
                           ALU.mult, None, koa3 + ["sm8"], ["xcb"])
                    TR([(PT[:, h * 128:(h + 1) * 128], ob[:, h * 128:(h + 1) * 128]) for h in range(4)], ["xcb"], ["PT"])
                    ACT(OT[:, :, tt * 128:(tt + 1) * 128], PT[:, 0:512].rearrange("p (h t) -> p h t", h=4), AF.Copy,
                        ["PT"], ["OT"] + CKEYS)
                wl = [None, None]
                for half in range(2):
                    slab_l, kl = next_slab("wlru", 0, 8, half * 512, 512)
                    slab_g1, kg1 = next_slab("win", 0, 8, GATE_OFF + half * 512, 512)
                    for mm_ in range(4):
                        m = half * 4 + mm_
                        b1, b2 = bank(), bank()
                        MM(PB[b1][:], [(slab_l[:, k, mm_ * 128:(mm_ + 1) * 128], ylw[:, k, :]) for k in range(8)],
                           [kl, "ylw"], [("pb", b1)])
                        MM(PB[b2][:], [(slab_g1[:, k, mm_ * 128:(mm_ + 1) * 128], hT[:, k, ws]) for k in range(8)],
                           [kg1, hTk(w)], [("pb", b2)])
                        s1, ks1 = tmp()
                        ACT(s1[:], PB[b2][:], AF.Tanh, [("pb", b2)], [ks1], scale=0.5)
                        STT(s1[:], s1[:], 1.0, PB[b1][:], ALU.add, ALU.mult, [ks1, ("pb", b1)], [ks1])
                        CP("pool", mergedT[:, m, :], s1[:], [ks1], [("mg", m)] + CKEYS)
                for half in range(2):
                    slab_a, ka_ = next_slab("wattn", 0, 4, half * 512, 512)
                    slab_g2, kg2 = next_slab("win", 0, 8, GATE_OFF + 1024 + half * 512, 512)
                    for mm_ in range(4):
                        m = half * 4 + mm_
                        b1, b2 = bank(), bank()
                        MM(PB[b1][:], [(slab_a[:, k, mm_ * 128:(mm_ + 1) * 128], OT[:, k, :]) for k in range(4)],
                           [ka_, "OT"], [("pb", b1)])
                        MM(PB[b2][:], [(slab_g2[:, k, mm_ * 128:(mm_ + 1) * 128], hT[:, k, ws]) for k in range(8)],
                           [kg2, hTk(w)], [("pb", b2)])
                        s2, ks2 = tmp()
                        ACT(s2[:], PB[b2][:], AF.Tanh, [("pb", b2)], [ks2], scale=0.5)
                        STT(s2[:], s2[:], 1.0, PB[b1][:], ALU.add, ALU.mult, [ks2, ("pb", b1)], [ks2])
                        TT("dve", mergedT[:, m, :], mergedT[:, m, :], s2[:], ALU.add, [ks2, ("mg", m)], [("mg", m)])
                slab_o = []
                for half in range(2):
                    slab_o.append(next_slab("wout", 0, 8, half * 512, 512))
                fbufF = fbuf[:].rearrange("p a b -> p (a b)")

                def e3_mm(tt):
                    tk = w * 4 + tt
                    tsl = slice(tt * 128, (tt + 1) * 128)
                    xb_ = xt[tt % 2]
                    kx = ("xt", tt % 2)
                    DMA("sp", xb_[:], x_d[b, tk * 128:(tk + 1) * 128, :], [], [kx])
                    bA, bB = bank_pair()
                    for half, bb in ((0, bA), (1, bB)):
                        so, kso = slab_o[half]
                        MM(PB[bb][:], [(mergedT[:, k, tsl], so[:, k, :]) for k in range(8)],
                           [kso] + [("mg", k) for k in range(8)], [("pb", bb)])
                    return bA, bB

                def e3_chain(tt, banks_):
                    tk = w * 4 + tt
                    tsl = slice(tt * 128, (tt + 1) * 128)
                    xb_ = xt[tt % 2]
                    kx = ("xt", tt % 2)
                    par = tt % 2
                    kpg = [("fbuf", 2 * par), ("fbuf", 2 * par + 1)]
                    for half, bb in enumerate(banks_):
                        hs = slice(half * 512, (half + 1) * 512)
                        ACT(xcb[:], PB[bb][:], AF.Square, [("pb", bb)], ["xcb", ("smx", half)], scale=0.5,
                            accum=sm[:, 2 + half:3 + half])
                        TT("dve", fbuf[:, 2 * par + half, :], PB[bb][:], gbc[:, 1, hs], ALU.mult, [("pb", bb), "gbc", ("smx", half)], [kpg[half]])
                    TT("dve", sm[:, 4:5], sm[:, 2:3], sm[:, 3:4], ALU.add, [("smx", 0), ("smx", 1)], ["sm4"])
                    rstd_from_ss(sm[:, 4:5], sm[:, 5:6], ["sm4"], "sm5", post_scale=0.5)
                    STT(x1t[:], fbufF[:, par * 1024:(par + 1) * 1024], sm[:, 5:6], xb_[:], ALU.mult, ALU.add,
                        kpg + ["sm5", kx], ["x1t"])
                    DMA("pool", x1_d[tk * 128:(tk + 1) * 128, :], x1t[:], ["x1t"], [("x1", tk)])
                    ACT(junk[:], x1t[:], AF.Square, ["x1t"], ["hb", "sm6"], accum=sm[:, 6:7])
                    rstd_from_ss(sm[:, 6:7], sm[:, 7:8], ["sm6"], "sm7")
                    STT(hb[:], x1t[:], sm[:, 7:8], gbc[:, 2, :], ALU.mult, ALU.mult, ["x1t", "sm7", "gbc"], ["hb"])
                    TR([(PT[:, k * 128:(k + 1) * 128], hb[:, k * 128:(k + 1) * 128]) for k in range(8)], ["hb"], ["PT"])
                    ACT(h2T[:, :, tsl], PT[:, :].rearrange("p (k t) -> p k t", k=8), AF.Copy, ["PT"],
                        ["h2T"] + CKEYS)

                pend3 = e3_mm(0)
                for tt in range(4):
                    nxt3 = e3_mm(tt + 1) if tt + 1 < 4 else None
                    e3_chain(tt, pend3)
                    pend3 = nxt3
                for pc in range(NFC // 2):
                    slab_gu, kgu = next_slab("wgu", 0, 8, pc * 512, 512)
                    for cc in range(2):
                        c = 2 * pc + cc
                        b1, b2 = bank(), bank()
                        MM(PB[b1][:], [(slab_gu[:, k, cc * 128:(cc + 1) * 128], h2T[:, k, :]) for k in range(8)],
                           [kgu, "h2T"], [("pb", b1)])
                        MM(PB[b2][:], [(slab_gu[:, k, 256 + cc * 128:256 + (cc + 1) * 128], h2T[:, k, :]) for k in range(8)],
                           [kgu, "h2T"], [("pb", b2)])
                        sg, ksg = tmp()
                        ACT(sg[:], PB[b1][:], AF.Tanh, [("pb", b1)], [ksg], scale=0.5)
                        STT(sg[:], sg[:], 1.0, PB[b1][:], ALU.add, ALU.mult, [ksg, ("pb", b1)], [ksg])
                        STT(aT[:, c, :], sg[:], 0.5, PB[b2][:], ALU.mult, ALU.mult, [ksg, ("pb", b2)],
                            [("aT", c)] + CKEYS)
                for nh in range(2):
                    hs = slice(nh * 512, (nh + 1) * 512)
                    b4 = [bank() for _ in range(4)]
                    for kh in range(2):
                        sK, kK = next_slab("wd", kh * 11, 11, nh * 512, 512)
                        for tt in range(4):
                            tsl = slice(tt * 128, (tt + 1) * 128)
                            MM(PB[b4[tt]][:], [(aT[:, kh * 11 + k, tsl], sK[:, k, :]) for k in range(11)],
                               [kK] + [("aT", kh * 11 + k) for k in range(11)], [("pb", b4[tt])],
                               first=(kh == 0), last=(kh == 1))
                    for tt in range(4):
                        tk = w * 4 + tt
                        bb = b4[tt]
                        col = 16 + tt * 2 + nh
                        ACT(junk[:, 0:512], PB[bb][:], AF.Square, [("pb", bb)], ["hb", ("smf", tt, nh)],
                            accum=sm[:, col:col + 1])
                        if nh == 0:
                            TT("dve", fbuf[:, tt, :], PB[bb][:], gbc[:, 3, 0:512], ALU.mult, [("pb", bb), "gbc", ("smf", tt, nh)], [("fbuf", tt)])
                        else:
                            ob_ = xt[tt % 2]
                            kx = ("xt", tt % 2)
                            TT("dve", ob_[:, 512:1024], PB[bb][:], gbc[:, 3, 512:1024], ALU.mult, [("pb", bb), "gbc", ("smf", tt, nh)], [kx])
                            TT("dve", sm[:, 4:5], sm[:, 16 + tt * 2:17 + tt * 2], sm[:, 17 + tt * 2:18 + tt * 2], ALU.add,
                               [("smf", tt, 0), ("smf", tt, 1)], ["sm4"])
                            rstd_from_ss(sm[:, 4:5], sm[:, 5:6], ["sm4"], "sm5")
                            DMA("sp", x1t[:], x1_d[tk * 128:(tk + 1) * 128, :], [("x1", tk)], ["x1t"])
                            ob_ = xt[tt % 2]
                            kx = ("xt", tt % 2)
                            STT(ob_[:, 0:512], fbuf[:, tt, :], sm[:, 5:6], x1t[:, 0:512], ALU.mult, ALU.add,
                                [("fbuf", tt), "sm5", "x1t"], [kx])
                            STT(ob_[:, 512:1024], ob_[:, 512:1024], sm[:, 5:6], x1t[:, 512:1024], ALU.mult, ALU.add,
                                [kx, "sm5", "x1t"], [kx])
                            DMA("pool", y_d[b, tk * 128:(tk + 1) * 128, :], ob_[:], [kx], [])
        S.finish()
        S.emit()
    return nc


def _tile_k(w):
    K, N = w.shape
    return np.ascontiguousarray(w.reshape(K // 128, 128, N).transpose(1, 0, 2))


def _interleave_gu(wg, wu):
    parts = []
    for pc in range(NFC // 2):
        parts.append(wg[:, pc * 256:(pc + 1) * 256])
        parts.append(wu[:, pc * 256:(pc + 1) * 256])
    return np.concatenate(parts, axis=1)


def _prep_shared(pre_mix_norm, w_in, conv_w, conv_b, w_rg, b_rg, w_ig, b_ig, lru_lambda, w_lru_proj,
                 w_attn_proj, w_out, post_mix_norm, pre_ffn_norm, w_ffn_gate, w_ffn_up, w_ffn_down, post_ffn_norm):
    f32 = np.float32
    W = np.asarray(w_in[0], f32)
    cols = []
    for j in range(8):
        cols.append(np.arange(128 * j, 128 * j + 128))
        cols.append(np.arange(1024 + 128 * j, 1024 + 128 * j + 128))
    for g in range(3):
        q0, k0, v0 = 2048 + g * 512, 3584 + g * 512, 5120 + g * 512
        cols.append(np.arange(q0, q0 + 512))
        cols.append(np.arange(k0, k0 + 512))
        cols.append(np.arange(v0, v0 + 512))
        for base0 in (q0, k0):
            for h in range(4):
                cols.append(np.arange(base0 + h * 128 + 16, base0 + h * 128 + 32))
                cols.append(np.arange(base0 + h * 128, base0 + h * 128 + 16))
    cols.append(np.arange(6656, 8704))
    cols = np.concatenate(cols)
    assert cols.shape[0] == NCOLS
    w_in_t = _tile_k(W[:, cols])
    gains = np.stack([np.asarray(a[0], f32) for a in (pre_mix_norm, post_mix_norm, pre_ffn_norm, post_ffn_norm)])
    gains_bc = np.ascontiguousarray(np.broadcast_to(gains[:, None, :], (4, 128, D)))
    fm = lambda v: np.asarray(v, f32).reshape(8, 128).T
    fvec = np.concatenate([fm(conv_w[0][j]) for j in range(4)] + [fm(conv_b[0]), fm(b_rg[0]), fm(b_ig[0]), fm(lru_lambda[0])], axis=1)
    fvec = np.ascontiguousarray(fvec, f32)

    def blk(wb):
        wb = np.asarray(wb[0], f32)
        out = np.zeros((128, 8, 128), f32)
        for j in range(8):
            out[0:64, j, 0:64] = wb[2 * j]
            out[64:128, j, 64:128] = wb[2 * j + 1]
        return out
    inv_freq = (np.float32(500000.0) ** (-(np.arange(0, 32, 2, dtype=f32)) / np.float32(32))).astype(f32)
    p = np.arange(128)
    rope_c = np.stack([inv_freq[p % 16], np.where((p % 32) < 16, -2.0, 2.0).astype(f32)], axis=1).astype(f32)
    return {
        "gains_bc": gains_bc, "w_in_t": w_in_t, "fvec": fvec, "rope_c": np.ascontiguousarray(rope_c),
        "wrg_blk": blk(w_rg), "wig_blk": blk(w_ig),
        "w_lru_t": _tile_k(np.asarray(w_lru_proj[0], f32)), "w_attn_t": _tile_k(np.asarray(w_attn_proj[0], f32)),
        "w_out_t": _tile_k(np.asarray(w_out[0], f32)), "w_gu_t": _tile_k(_interleave_gu(np.asarray(w_ffn_gate[0], f32), np.asarray(w_ffn_up[0], f32))),
        "w_down_t": _tile_k(np.asarray(w_ffn_down[0], f32)),
    }


def make_in_maps(x, positions, shared, n_cores):
    B = x.shape[0]
    per = B // n_cores
    maps = []
    for c in range(n_cores):
        pos = np.asarray(positions[c * per:(c + 1) * per], np.int32)
        m = dict(shared)
        m["x"] = np.ascontiguousarray(np.asarray(x[c * per:(c + 1) * per], np.float32))
        m["pos_bc"] = np.ascontiguousarray(np.broadcast_to(pos[:, None, :], (per, 128, SEQ)))
        maps.append(m)
    return maps


def kernel(x, positions, pre_mix_norm, w_in, conv_w, conv_b, w_rg, b_rg, w_ig, b_ig, lru_lambda, w_lru_proj,
           w_attn_proj, w_out, post_mix_norm, pre_ffn_norm, w_ffn_gate, w_ffn_up, w_ffn_down, post_ffn_norm):
    x = np.asarray(x)
    shared = _prep_shared(pre_mix_norm, w_in, conv_w, conv_b, w_rg, b_rg, w_ig, b_ig, lru_lambda, w_lru_proj,
                          w_attn_proj, w_out, post_mix_norm, pre_ffn_norm, w_ffn_gate, w_ffn_up, w_ffn_down, post_ffn_norm)
    maps = make_in_maps(x, np.asarray(positions), shared, NCORES)
    nc = build(x.shape[0] // NCORES)
    res = run_bass_kernel_spmd(nc, maps, core_ids=list(range(NCORES)))
    return np.concatenate([np.asarray(r["y"], np.float32) for r in res.results], axis=0)
```

```python
import math
from contextlib import ExitStack

import numpy as np
import concourse.bass as bass
import concourse.mybir as mybir
from concourse.bass_utils import run_bass_kernel_spmd

F32 = mybir.dt.float32
BF16 = mybir.dt.bfloat16
I32 = mybir.dt.int32
ALU = mybir.AluOpType
AF = mybir.ActivationFunctionType

ENGS = ("pe", "act", "dve", "pool", "sp")
NSLOT = 8

D = 1024
SEQ = 2048
NCORES = 8
D_FF = 2816
NFC = D_FF // 128
EPS = 1e-6
GROUP_DIL = (1, 4, 16)
LRU_COLS = 2048
GRP_COLS = 1792
GATE_OFF = LRU_COLS + 3 * GRP_COLS
NCOLS = GATE_OFF + 2048
OAW = 516


class Sched:
    def __init__(self, nc, same_eng_sync=True):
        self.nc = nc
        self.same_eng_sync = same_eng_sync
        self.lists = {e: [] for e in ENGS}
        self.count = {e: 0 for e in ENGS}
        self.waited = {e: {} for e in ENGS}
        self.last_w = {}
        self.readers = {}
        self.slot_uses = {}
        self.slot_next = {e: 0 for e in ENGS}
        self.sems = {}
        self.sem_names = [f"s_{e}" for e in ENGS if e != "sp"]

    def _deps(self, reads, writes):
        deps = {}
        for k in reads:
            t = self.last_w.get(k)
            if t is not None:
                deps[t[0]] = max(deps.get(t[0], 0), t[1])
        for k in writes:
            t = self.last_w.get(k)
            if t is not None:
                deps[t[0]] = max(deps.get(t[0], 0), t[1])
            for t in self.readers.get(k, ()):
                deps[t[0]] = max(deps.get(t[0], 0), t[1])
        return deps

    def _emit_waits(self, eng, deps):
        own = f"s_{eng}"
        for sem, val in deps.items():
            if sem == own and (eng == "pe" or not self.same_eng_sync):
                continue
            if self.waited[eng].get(sem, 0) >= val:
                continue
            self.waited[eng][sem] = val
            self.lists[eng].append(("wait", sem, val))

    def _record(self, token, reads, writes):
        for k in writes:
            self.last_w[k] = token
            self.readers[k] = []
        for k in reads:
            self.readers.setdefault(k, []).append(token)

    def op(self, eng, fn, reads=(), writes=()):
        self._emit_waits(eng, self._deps(reads, writes))
        self.count[eng] += 1
        token = (f"s_{eng}", self.count[eng])
        self.lists[eng].append(("op", fn, f"s_{eng}", 1))
        self._record(token, reads, writes)

    def dma(self, q, fn, reads=(), writes=()):
        slot = self.slot_next[q]
        self.slot_next[q] = (slot + 1) % NSLOT
        sem = f"d_{q}_{slot}"
        uses = self.slot_uses.get(sem, 0)
        deps = self._deps(reads, writes)
        if uses > 0:
            deps[sem] = max(deps.get(sem, 0), 16 * uses)
        self._emit_waits(q, deps)
        self.slot_uses[sem] = uses + 1
        token = (sem, 16 * (uses + 1))
        self.lists[q].append(("op", fn, sem, 16))
        self._record(token, reads, writes)

    def finish(self):
        for sem, uses in self.slot_uses.items():
            q = sem.split("_")[1]
            self._emit_waits(q, {sem: 16 * uses})

    def emit(self):
        nc = self.nc
        names = list(self.sem_names) + sorted(self.slot_uses.keys())
        with ExitStack() as st:
            for n in names:
                self.sems[n] = st.enter_context(nc.semaphore(n))
            block = st.enter_context(nc.Block())

            def replay(ename):
                def body(e):
                    for item in self.lists[ename]:
                        if item[0] == "wait":
                            e.wait_ge(self.sems[item[1]], item[2])
                        else:
                            item[1](e).then_inc(self.sems[item[2]], item[3])
                return body

            for ename, reg in (("sp", block.sync), ("pe", block.tensor), ("act", block.scalar),
                               ("dve", block.vector), ("pool", block.gpsimd)):
                if self.lists[ename]:
                    reg(replay(ename))


def build(nseq, stop=None):
    nc = bass.Bass("TRN2", target_bir_lowering=False)
    dt_in = lambda n, s, d=F32: nc.dram_tensor(n, s, d, kind="ExternalInput").ap()
    x_d = dt_in("x", [nseq, SEQ, D])
    pos_d = dt_in("pos_bc", [nseq, 128, SEQ], I32)
    gains_d = dt_in("gains_bc", [4, 128, D])
    win_d = dt_in("w_in_t", [128, 8, NCOLS])
    fvec_d = dt_in("fvec", [128, 64])
    ropec_d = dt_in("rope_c", [128, 2])
    wrg_d = dt_in("wrg_blk", [128, 8, 128])
    wig_d = dt_in("wig_blk", [128, 8, 128])
    wlru_d = dt_in("w_lru_t", [128, 8, D])
    wattn_d = dt_in("w_attn_t", [128, 4, D])
    wout_d = dt_in("w_out_t", [128, 8, D])
    wlg_d = dt_in("w_lg_t", [128, 8, 2048])
    wag_d = dt_in("w_ag_t", [128, 8, 2048])
    wgu_d = dt_in("w_gu_t", [128, 8, 2 * D_FF])
    wd_d = dt_in("w_down_t", [128, NFC, D])
    y_d = nc.dram_tensor("y", [nseq, SEQ, D], F32, kind="ExternalOutput").ap()
    oa_d = nc.dram_tensor("oa_scr", [3, SEQ, OAW], F32, kind="ExternalOutput").ap()
    yl_d = nc.dram_tensor("yl_scr", [D, SEQ], BF16, kind="ExternalOutput").ap()
    x1_d = nc.dram_tensor("x1_scr", [SEQ, D], F32, kind="ExternalOutput").ap()

    wsrc = {"win": win_d, "wlru": wlru_d, "wattn": wattn_d, "wout": wout_d, "wgu": wgu_d, "wd": wd_d, "wlg": wlg_d, "wag": wag_d}
    wbf = {k: nc.dram_tensor(k + "_bf", list(v.shape), BF16, kind="ExternalOutput").ap() for k, v in wsrc.items()}

    with ExitStack() as st:
        sb = lambda n, s, d: st.enter_context(nc.sbuf_tensor(n, s, d))
        S = Sched(nc)
        gbc = sb("gbc", [128, 4, D], F32)
        C4 = sb("C4", [128, SEQ], F32)
        S4 = sb("S4", [128, SEQ], F32)
        hT = sb("hT", [128, 8, SEQ], BF16)
        ylw = sb("ylw", [128, 8, 512], BF16)
        arena = sb("arena", [128, 24832], BF16)
        wsl = [sb(f"wsl{i}", [128, 11, 512], BF16) for i in range(2)]
        wsw = sb("wsw", [128, 8, 128], BF16)
        wrg = sb("wrg", [128, 8, 128], BF16)
        wig = sb("wig", [128, 8, 128], BF16)
        xt = [sb(f"xt{i}", [128, D], F32) for i in range(2)]
        NT = 8
        T = [sb(f"T{i}", [128, 512], F32) for i in range(NT)]
        xre = [sb(f"xre{i}", [128, 515], F32) for i in range(2)]
        hb = sb("hb", [128, D], BF16)
        junk = hb
        fbuf = sb("fbuf", [128, 4, 512], F32)
        Pc = sb("Pc", [128, 512], BF16)
        Pp = sb("Pp", [128, 512], BF16)
        xcb = sb("xcb", [128, 512], BF16)
        ob = xcb
        xcb2 = sb("xcb2", [128, 512], BF16)
        oe = sb("oe", [128, OAW], F32)
        oa3_a = sb("oa3", [128, 3, OAW], F32)
        x1t = sb("x1t", [128, D], F32)
        posi = sb("posi", [128, 512], I32)
        ident = sb("ident", [128, 128], BF16)
        mcur = sb("mcur", [128, 4, 128], BF16)
        mprev = sb("mprev", [128, 4, 128], BF16)
        ones1 = sb("ones1", [128, 1], BF16)
        fvec = sb("fvec_sb", [128, 64], F32)
        fder = sb("fder", [128, 40], F32)
        ropec = sb("ropec", [128, 2], F32)
        sm = sb("sm", [128, 32], F32)
        hprev = sb("hprev", [128, 1], F32)
        halos = sb("halos", [128, 4, 3], F32)
        qT = arena[:, 0:8192].rearrange("p (h t) -> p h t", h=4)
        kT = arena[:, 8192:16384].rearrange("p (h t) -> p h t", h=4)
        V = arena[:, 16384:24576].rearrange("p (b c) -> p b c", b=16)
        OT = arena[:, 0:2048].rearrange("p (h t) -> p h t", h=4)
        mergedT = arena[:, 2048:6144].rearrange("p (k t) -> p k t", k=8)
        h2T = arena[:, 6144:10240].rearrange("p (k t) -> p k t", k=8)
        aT = arena[:, 10240:21504].rearrange("p (k t) -> p k t", k=NFC)
        PB = [st.enter_context(nc.psum_tensor(f"pb{i}", [128, 512], F32)) for i in range(7)]
        PT = st.enter_context(nc.psum_tensor("pt", [128, 1024], BF16))
        bank_ctr = [0]

        def bank():
            b = bank_ctr[0] % 6
            bank_ctr[0] += 1
            return b

        def bank_pair():
            if bank_ctr[0] % 2:
                bank_ctr[0] += 1
            b = bank_ctr[0] % 6
            bank_ctr[0] += 2
            return b, b + 1

        tctr = [0]

        def tmp():
            i = tctr[0] % NT
            tctr[0] += 1
            return T[i], ("T", i)

        def ACT(out, in_, func, r, w, scale=1.0, bias=0.0, accum=None):
            S.op("act", lambda e: e.activation(out, in_, func, bias=bias, scale=scale, accum_out=accum), r, w)

        def TS(eng, out, in0, s1, s2, op0, op1, r, w):
            if s2 is None:
                S.op(eng, lambda e: e.tensor_scalar(out, in0, s1, None, op0), r, w)
            else:
                S.op(eng, lambda e: e.tensor_scalar(out, in0, s1, s2, op0, op1), r, w)

        def TT(eng, out, in0, in1, op, r, w):
            S.op(eng, lambda e: e.tensor_tensor(out, in0, in1, op), r, w)

        def STT(out, in0, sc, in1, op0, op1, r, w):
            S.op("dve", lambda e: e.scalar_tensor_tensor(out, in0, sc, in1, op0, op1), r, w)

        def CP(eng, out, in_, r, w):
            S.op(eng, lambda e: e.tensor_copy(out, in_), r, w)

        def DMA(q, out, in_, r, w):
            S.dma(q, lambda e: e.dma_start(out=out, in_=in_), r, w)

        def MM(out, pairs, r, w, first=True, last=True):
            def fn(e):
                n = len(pairs)
                for i, (l, rr) in enumerate(pairs):
                    ins = e.matmul(out, l, rr, start=(first and i == 0), stop=(last and i == n - 1))
                return ins
            S.op("pe", fn, r, w)

        def TR(outs_ins, r, w):
            def fn(e):
                for o, i in outs_ins:
                    ins = e.transpose(o, i, ident[:])
                return ins
            S.op("pe", fn, list(r) + ["ident"], w)

        slab_ctr = [0]

        def next_slab(name, k0, nk, c0, ncol):
            i = slab_ctr[0] % 2
            slab_ctr[0] += 1
            DMA("sp", wsl[i][:, 0:nk, 0:ncol], wbf[name][:, k0:k0 + nk, c0:c0 + ncol],
                [("wbf", name, kc, 1 if (name == "win" and c0 >= LRU_COLS) else 0) for kc in range(k0, k0 + nk)], [("wsl", i)])
            return wsl[i], ("wsl", i)

        def rstd_from_ss(ss_ap, dst_ap, keys_r, key_w, post_scale=None):
            TS("dve", dst_ap, ss_ap, 1.0 / D, EPS, ALU.mult, ALU.add, keys_r, [key_w])
            ACT(dst_ap, dst_ap, AF.Sqrt, [key_w], [key_w])
            S.op("dve", lambda e: e.reciprocal(dst_ap, dst_ap), [key_w], [key_w])
            if post_scale is not None:
                TS("dve", dst_ap, dst_ap, post_scale, None, ALU.mult, None, [key_w], [key_w])

        DMA("sp", gbc[:], gains_d.rearrange("g p d -> p g d"), [], ["gbc"])
        DMA("sp", fvec[:], fvec_d, [], ["fvec"])
        DMA("sp", ropec[:], ropec_d, [], ["ropec"])
        DMA("pool", wrg[:], wrg_d, [], ["wrg"])
        DMA("pool", wig[:], wig_d, [], ["wig"])
        S.op("pool", lambda e: e.memset(ident[:], 0.0), [], ["ident"])
        S.op("pool", lambda e: e.affine_select(ident[:], ident[:], [[1, 128]], ALU.not_equal, 1.0,
                                               base=0, channel_multiplier=-1), ["ident"], ["ident"])
        S.op("pool", lambda e: e.memset(mcur[:], 1.0), [], ["mcur"])
        S.op("pool", lambda e: e.affine_select(mcur[:], mcur[:], [[0, 4], [1, 128]], ALU.is_ge, 0.0,
                                               base=0, channel_multiplier=-1), ["mcur"], ["mcur"])
        S.op("pool", lambda e: e.memset(mprev[:], 1.0), [], ["mprev"])
        S.op("pool", lambda e: e.affine_select(mprev[:], mprev[:], [[0, 4], [-1, 128]], ALU.is_ge, 0.0,
                                               base=0, channel_multiplier=1), ["mprev"], ["mprev"])
        S.op("pool", lambda e: e.memset(ones1[:], 1.0), [], ["ones1"])
        TS("pool", mcur[:], mcur[:], 30000.0, -30000.0, ALU.mult, ALU.add, ["mcur"], ["mcur"])
        TS("pool", mprev[:], mprev[:], 30000.0, -30000.0, ALU.mult, ALU.add, ["mprev"], ["mprev"])
        TS("dve", fder[:, 0:16], fvec[:, 40:56], 0.5, None, ALU.mult, None, ["fvec"], ["fder"])
        ACT(fder[:, 32:40], fvec[:, 56:64], AF.Exp, ["fvec", "fder"], ["fder"], scale=-1.0)
        ACT(fder[:, 32:40], fder[:, 32:40], AF.Ln, ["fder"], ["fder"], bias=1.0)
        TS("dve", fder[:, 16:24], fder[:, 32:40], -8.0, None, ALU.mult, None, ["fder"], ["fder"])
        TS("dve", fder[:, 24:32], fder[:, 32:40], -4.0, None, ALU.mult, None, ["fder"], ["fder"])
        cw = lambda j, c: fvec[:, j * 8 + c:j * 8 + c + 1]
        cb = lambda c: fvec[:, 32 + c:33 + c]
        hbrg = lambda c: fder[:, c:c + 1]
        hbig = lambda c: fder[:, 8 + c:9 + c]
        c1v = lambda c: fder[:, 16 + c:17 + c]
        hc1 = lambda c: fder[:, 24 + c:25 + c]
        QS = 1.0 / math.sqrt(128.0)
        hTk = lambda w: ("hT", w)
        ALLHT = [hTk(w) for w in range(4)]
        CKEYS = [("qT", h) for h in range(4)] + [("kT", h) for h in range(4)] + [("V", bl) for bl in range(16)]
        EKEYS = ["OT", "h2T"] + [("mg", m) for m in range(8)] + [("aT", c) for c in range(NFC)]

        for kc in range(8):
            S.dma("pool", lambda e, o=wbf["win"][:, kc, 0:LRU_COLS], i=win_d[:, kc, 0:LRU_COLS]: e.dma_start(out=o, in_=i),
                  [], [("wbf", "win", kc, 0)])
        for kc in range(8):
            S.dma("pool", lambda e, o=wbf["win"][:, kc, LRU_COLS:NCOLS], i=win_d[:, kc, LRU_COLS:NCOLS]: e.dma_start(out=o, in_=i),
                  [], [("wbf", "win", kc, 1)])
        for k_, v_ in wsrc.items():
            if k_ == "win":
                continue
            for kc in range(v_.shape[1]):
                S.dma("pool", lambda e, o=wbf[k_][:, kc, :], i=v_[:, kc, :]: e.dma_start(out=o, in_=i), [], [("wbf", k_, kc, 0)])

        for b in range(nseq):
            for w in range(4):
                ws = slice(w * 512, (w + 1) * 512)
                DMA("sp", posi[:], pos_d[b, :, ws], [], ["posi"])
                ang, ka = tmp()
                CP("dve", ang[:], posi[:], ["posi"], [ka])
                TS("dve", ang[:], ang[:], ropec[:, 0:1], None, ALU.mult, None, [ka, "ropec"], [ka])
                TS("dve", posi[:], ang[:], 1.0 / (2 * math.pi), None, ALU.mult, None, [ka], ["posi"])
                kf, kk = tmp()
                CP("dve", kf[:], posi[:], ["posi"], [kk])
                STT(ang[:], kf[:], -2 * math.pi, ang[:], ALU.mult, ALU.add, [kk, ka], [ka])
                sh, ksh = tmp()
                ch, kch = tmp()
                ACT(sh[:], ang[:], AF.Sin, [ka], [ksh], scale=0.5)
                ACT(ch[:], ang[:], AF.Sin, [ka], [kch], scale=-0.5, bias=math.pi / 2)
                TT("dve", ch[:], sh[:], ch[:], ALU.mult, [ksh, kch], [kch])
                TS("dve", S4[:, ws], ch[:], ropec[:, 1:2], None, ALU.mult, None, [kch, "ropec"], [("S4", w)])
                TT("dve", sh[:], sh[:], sh[:], ALU.mult, [ksh], [ksh])
                TS("dve", C4[:, ws], sh[:], -2.0, 1.0, ALU.mult, ALU.add, [ksh], [("C4", w)])

            if stop == 'R':
                break
            for t in range(16):
                xb_ = xt[t % 2]
                kx = ("xt", t % 2)
                DMA("sp", xb_[:], x_d[b, t * 128:(t + 1) * 128, :], [], [kx])
                ACT(junk[:], xb_[:], AF.Square, [kx], ["hb", "sm0"], accum=sm[:, 0:1])
                rstd_from_ss(sm[:, 0:1], sm[:, 1:2], ["sm0"], "sm1")
                STT(hb[:], xb_[:], sm[:, 1:2], gbc[:, 0, :], ALU.mult, ALU.mult, [kx, "sm1", "gbc"], ["hb"])
                TR([(PT[:, k * 128:(k + 1) * 128], hb[:, k * 128:(k + 1) * 128]) for k in range(8)], ["hb"], ["PT"])
                ACT(hT[:, :, t * 128:(t + 1) * 128], PT[:, :].rearrange("p (k t) -> p k t", k=8), AF.Copy,
                    ["PT"], [hTk(t // 4)])

            if stop == 'A':
                break
            items = [(j, w) for j in range(8) for w in range(4)]
            bslabs = {}
            bufsets = [
                [(T[k][:], ("T", k)) for k in range(8)],
                [(fbuf[:, k, :], ("fbuf", k)) for k in range(4)] + [(x1t[:, 0:512], "x1t"), (x1t[:, 512:1024], "x1t"),
                                                                      (xt[0][:, 0:512], ("xt", 0)), (xt[0][:, 512:1024], ("xt", 0))],
            ]
            bbank = [0]

            def bbk():
                bbank[0] += 1
                return bbank[0] % 7

            def chain(i, slot):
                j, w = items[i]
                s, jj = j // 2, j % 2
                if jj == 0 and w == 0:
                    bslabs[s] = next_slab("win", 0, 8, s * 512, 512)
                slab, kslab = bslabs[s]
                ws = slice(w * 512, (w + 1) * 512)
                (gr, kgr), (xc, kxc), (tr_, ktr), (ti_, kti), (a_, kaa), (a2, ka2), (hh, khh), (p2, kp2) = bufsets[slot]
                bx, bg = bbk(), bbk()
                MM(PB[bx][:], [(slab[:, k, (2 * jj) * 128:(2 * jj + 1) * 128], hT[:, k, ws]) for k in range(8)],
                   [kslab, hTk(w)], [("pb", bx)])
                MM(PB[bg][:], [(slab[:, k, (2 * jj + 1) * 128:(2 * jj + 2) * 128], hT[:, k, ws]) for k in range(8)],
                   [kslab, hTk(w)], [("pb", bg)])
                yield
                xe = xre[w % 2]
                kxe = ("xre", w % 2)
                ACT(xe[:, 3:515], PB[bx][:], AF.Copy, [("pb", bx)], [kxe])
                ACT(gr, PB[bg][:], AF.Copy, [("pb", bg)], [kgr])
                CP("dve", halos[:, w, :], xe[:, 512:515], [kxe], [("halo", w)])
                yield
                if w == 0:
                    S.op("pool", lambda e, xe=xe: e.memset(xe[:, 0:3], 0.0), [], [kxe])
                else:
                    CP("dve", xe[:, 0:3], halos[:, w - 1, :], [("halo", w - 1)], [kxe])
                TT("pool", p2, gr, gr, ALU.mult, [kgr], [kp2])
                yield
                TS("dve", xc, xe[:, 0:512], cw(0, j), cb(j), ALU.mult, ALU.add, [kxe, "fvec"], [kxc])
                TS("pool", p2, p2, 0.044715, 1.0, ALU.mult, ALU.add, [kp2], [kp2])
                yield
                for q in range(1, 4):
                    STT(xc, xe[:, q:q + 512], cw(q, j), xc, ALU.mult, ALU.add, [kxe, kxc, "fvec"], [kxc])
                    if q == 1:
                        TT("pool", p2, p2, gr, ALU.mult, [kp2, kgr], [kp2])
                    yield
                xb2, kxb2 = (xcb, "xcb") if slot == 0 else (xcb2, "xcb2")
                ACT(xb2[:], xc, AF.Copy, [kxc], [kxb2])
                yield
                br, bi = bbk(), bbk()
                MM(PB[br][:], [(wrg[:, j, :], xb2[:])], ["wrg", kxb2], [("pb", br)])
                MM(PB[bi][:], [(wig[:, j, :], xb2[:])], ["wig", kxb2], [("pb", bi)])
                yield
                ACT(tr_, PB[br][:], AF.Tanh, [("pb", br), "fder"], [ktr], scale=0.5, bias=hbrg(j))
                yield
                ACT(ti_, PB[bi][:], AF.Tanh, [("pb", bi), "fder"], [kti], scale=0.5, bias=hbig(j))
                yield
                ACT(a_, tr_, AF.Exp, [ktr, "fder"], [kaa], scale=hc1(j), bias=hc1(j))
                yield
                ACT(tr_, tr_, AF.Tanh, [ktr, "fder"], [ktr], scale=hc1(j), bias=hc1(j))
                yield
                ACT(p2, p2, AF.Tanh, [kp2], [kp2], scale=0.7978845608028654)
                TT("dve", a2, a_, a_, ALU.mult, [kaa], [ka2])
                yield
                STT(a2, a2, 1.0, tr_, ALU.add, ALU.mult, [ka2, ktr], [ka2])
                yield
                ACT(a2, a2, AF.Sqrt, [ka2], [ka2], scale=-0.25)
                if w == 0:
                    S.op("pool", lambda e, a2=a2: e.memset(a2[:, 0:1], 0.5), [ka2], [ka2])
                STT(ti_, ti_, 1.0, xc, ALU.add, ALU.mult, [kti, kxc], [kti])
                yield
                TT("dve", ti_, ti_, a2, ALU.mult, [kti, ka2], [kti])
                STT(p2, p2, 1.0, gr, ALU.add, ALU.mult, [kp2, kgr], [kp2])
                yield
                if w == 0:
                    S.op("dve", lambda e, hh=hh, a_=a_, ti_=ti_: e.tensor_tensor_scan(
                        hh, a_, ti_, 0.0, ALU.mult, ALU.add), [kaa, kti], [khh])
                else:
                    S.op("dve", lambda e, hh=hh, a_=a_, ti_=ti_: e.tensor_tensor_scan(
                        hh, a_, ti_, hprev[:, 0:1], ALU.mult, ALU.add), [kaa, kti, "hprev"], [khh])
                CP("dve", hprev[:], hh[:, 511:512], [khh], ["hprev"])
                yield
                STT(ylw[:, slot, :], p2, 0.5, hh, ALU.mult, ALU.mult, [kp2, khh], [("ylws", slot), "ylw"])
                DMA("pool", yl_d[j * 128:(j + 1) * 128, ws], ylw[:, slot, :], [("ylws", slot)], [("yl", w)])
                yield

            for i in range(0, len(items), 2):
                gens = [chain(i, 0), chain(i + 1, 1)]
                alive = [True, True]
                while any(alive):
                    for gi, gen in enumerate(gens):
                        if alive[gi]:
                            try:
                                next(gen)
                            except StopIteration:
                                alive[gi] = False

            if stop == 'B':
                break
            for g in range(3):
                d = GROUP_DIL[g]
                nb = 16 // d
                base = LRU_COLS + g * GRP_COLS
                for which in range(2):
                    DMA("sp", wsw[:], wbf["win"][:, :, base + 1536 + which * 128:base + 1664 + which * 128], [("wbf", "win", kc, 1) for kc in range(8)], ["wsw"])
                    slab, kslab = next_slab("win", 0, 8, base + which * 512, 512)
                    dstT = qT if which == 0 else kT
                    dname = "qT" if which == 0 else "kT"
                    sc = QS if which == 0 else 1.0
                    for w in range(4):
                        ws = slice(w * 512, (w + 1) * 512)
                        bs0, bs1 = bank(), bank()
                        MM(PB[bs0][0:64, :], [(wsw[:, k, 0:64], hT[:, k, ws]) for k in range(8)],
                           ["wsw", hTk(w)], [("pb", bs0)])
                        MM(PB[bs1][0:64, :], [(wsw[:, k, 64:128], hT[:, k, ws]) for k in range(8)],
                           ["wsw", hTk(w)], [("pb", bs1)])
                        n_i = 512 // d
                        for h in range(4):
                            bm = bank()
                            MM(PB[bm][:], [(slab[:, k, h * 128:(h + 1) * 128], hT[:, k, ws]) for k in range(8)],
                               [kslab, hTk(w)], [("pb", bm)])
                            kd = (dname, h)
                            dst_full = dstT[:, h, :].rearrange("p (r i) -> p i r", r=d)[:, w * n_i:(w + 1) * n_i, :]
                            dst_rope = dstT[0:32, h, :].rearrange("p (r i) -> p i r", r=d)[:, w * n_i:(w + 1) * n_i, :]
                            src_full = PB[bm][:, :].rearrange("p (i r) -> p i r", r=d)
                            stg, kst = tmp()
                            ACT(stg[:], PB[bm][:], AF.Copy, [("pb", bm)], [kst])
                            if stop != 'C1a':
                                t1, k1 = tmp()
                                t2, k2 = tmp()
                                bs = bs0 if h < 2 else bs1
                                po = 32 * (h % 2)
                                TT("dve", t1[0:32, :], PB[bs][po:po + 32, :], S4[po:po + 32, ws], ALU.mult,
                                   [("pb", bs), ("S4", w)], [k1])
                                TT("dve", t2[0:32, :], stg[0:32, :], C4[0:32, ws], ALU.mult, [kst, ("C4", w)], [k2])
                                TT("dve", stg[0:32, :], t1[0:32, :], t2[0:32, :], ALU.add, [k1, k2, kst], [kst])
                            ACT(dstT[:, h, ws], stg[:], AF.Copy, [kst], [kd] + EKEYS, scale=sc)
                if stop in ('C1', 'C1a'):
                    break
                slab, kslab = next_slab("win", 0, 8, base + 1024, 512)
                for blk in range(16):
                    r_, n_ = blk // nb, blk % nb
                    st0 = r_ + d * 128 * n_
                    bv = bank()
                    MM(PB[bv][:], [(hT[:, k, st0:st0 + 127 * d + 1:d], slab[:, k, 0:512]) for k in range(8)],
                       [kslab] + ALLHT, [("pb", bv)])
                    ACT(V[:, blk, :], PB[bv][:], AF.Copy, [("pb", bv)], [("V", blk)] + EKEYS)
                if stop == 'C2':
                    break
                qk_r = [("qT", h) for h in range(4)] + [("kT", h) for h in range(4)]

                def emit_S(blk):
                    r_, n_ = blk // nb, blk % nb
                    has_prev = n_ > 0
                    st0 = r_ + d * 128 * n_
                    cs = slice(st0, st0 + 127 * d + 1, d)
                    ps_ = slice(st0 - 128 * d, st0 - d + 1, d)
                    outs = []
                    for ks, negm, kneg in ((cs, mcur, "mcur"),) + (((ps_, mprev, "mprev"),) if has_prev else ()):
                        bk = bank()

                        def s_fn(e, bk=bk, cs=cs, ks=ks, negm=negm):
                            ins = e.matmul(PB[bk][:], ident[:], negm[:].rearrange("p h q -> p (h q)"), start=True, stop=False)
                            for h in range(4):
                                ins = e.matmul(PB[bk][:, h * 128:(h + 1) * 128], kT[:, h, ks], qT[:, h, cs],
                                               start=False, stop=(h == 3))
                            return ins
                        S.op("pe", s_fn, qk_r + ["ident", kneg], [("pb", bk)])
                        outs.append(bk)
                    return outs

                def emit_rest(blk, banks_):
                    r_, n_ = blk // nb, blk % nb
                    has_prev = n_ > 0
                    par = blk % 2
                    Pc_, kPc = ylw[:, 2 + 2 * par, :], ("ylwp", 2 * par)
                    Pp_, kPp = ylw[:, 3 + 2 * par, :], ("ylwp", 2 * par + 1)
                    ACT(Pc_, PB[banks_[0]][:], AF.Exp, [("pb", banks_[0])], [kPc, "ylw"])
                    if has_prev:
                        ACT(Pp_, PB[banks_[1]][:], AF.Exp, [("pb", banks_[1])], [kPp, "ylw"])
                    bo = bank()

                    def pv_fn(e, bo=bo, blk=blk, has_prev=has_prev, Pc_=Pc_, Pp_=Pp_):
                        for h in range(4):
                            hs = slice(h * 128, (h + 1) * 128)
                            ins = e.matmul(PB[bo][:, hs], Pc_[:, hs], V[:, blk, hs], start=True, stop=not has_prev)
                            if has_prev:
                                ins = e.matmul(PB[bo][:, hs], Pp_[:, hs], V[:, blk - 1, hs], start=False, stop=True)
                        for h in range(4):
                            hs = slice(h * 128, (h + 1) * 128)
                            ins = e.matmul(PB[6][:, h:h + 1], Pc_[:, hs], ones1[:], start=True, stop=not has_prev)
                            if has_prev:
                                ins = e.matmul(PB[6][:, h:h + 1], Pp_[:, hs], ones1[:], start=False, stop=True)
                        return ins
                    rd = [kPc, ("V", blk), "ones1"] + ([kPp, ("V", blk - 1)] if has_prev else [])
                    S.op("pe", pv_fn, rd, [("pb", bo), ("pb", 6)])
                    oe_, koe = (oe, "oe") if par == 0 else (x1t, "x1t")
                    ACT(oe_[:, 0:512], PB[bo][:], AF.Copy, [("pb", bo)], [koe])
                    CP("dve", oe_[:, 512:516], PB[6][:, 0:4], [("pb", 6)], [koe])
                    st0 = r_ + d * 128 * n_
                    DMA("pool", oa_d[g, st0:st0 + 127 * d + 1:d, :], oe_[:, 0:OAW], [koe], [("oa", g)])

                pend = emit_S(0)
                for blk in range(16):
                    nxt = emit_S(blk + 1) if blk + 1 < 16 else None
                    emit_rest(blk, pend)
                    pend = nxt

            if stop in ('C', 'C1', 'C2', 'C1a'):
                break
            for w in range(4):
                ws = slice(w * 512, (w + 1) * 512)
                DMA("sp", ylw[:], yl_d[:, ws].rearrange("(k p) t -> p k t", p=128), [("yl", w)], [("ylws", 0), ("ylws", 1), "ylw"] + [("ylwp", k) for k in range(4)])
                for tt in range(4):
                    tk = w * 4 + tt
                    if tt % 2 == 0:
                        oa3, koa3 = oa3_a, ["oa3"]
                    else:
                        oa3 = fbuf[:].rearrange("p a b -> p (a b)")[:, 0:3 * OAW].rearrange("p (g c) -> p g c", g=3)
                        koa3 = [("fbuf", k) for k in range(4)]
                    DMA("sp", oa3[:], oa_d[:, tk * 128:(tk + 1) * 128, :].rearrange("g t c -> t g c"),
                        [("oa", 0), ("oa", 1), ("oa", 2)], koa3)
                    TT("dve", oa3[:, 0, :], oa3[:, 0, :], oa3[:, 1, :], ALU.add, koa3, koa3)
                    TT("dve", oa3[:, 0, :], oa3[:, 0, :], oa3[:, 2, :], ALU.add, koa3, koa3)
                    S.op("dve", lambda e, oa3=oa3: e.reciprocal(sm[:, 8:12], oa3[:, 0, 512:516]), koa3, ["sm8"])
                    for h in range(4):
                        TS("dve", ob[:, h * 128:(h + 1) * 128], oa3[:, 0, h * 128:(h + 1) * 128], sm[:, 8 + h:9 + h], None,
                           ALU.mult, None, koa3 + ["sm8"], ["xcb"])
                    TR([(PT[:, h * 128:(h + 1) * 128], ob[:, h * 128:(h + 1) * 128]) for h in range(4)], ["xcb"], ["PT"])
                    ACT(OT[:, :, tt * 128:(tt + 1) * 128], PT[:, 0:512].rearrange("p (h t) -> p h t", h=4), AF.Copy,
                        ["PT"], ["OT"] + CKEYS)
                for qd in range(4):
                    slab_l, kl = next_slab("wlg", 0, 8, qd * 512, 512)
                    for mm_ in range(2):
                        m = qd * 2 + mm_
                        b1, b2 = bank(), bank()
                        MM(PB[b1][:], [(slab_l[:, k, mm_ * 128:(mm_ + 1) * 128], ylw[:, k, :]) for k in range(8)],
                           [kl, "ylw"], [("pb", b1)])
                        MM(PB[b2][:], [(slab_l[:, k, 256 + mm_ * 128:256 + (mm_ + 1) * 128], hT[:, k, ws]) for k in range(8)],
                           [kl, hTk(w)], [("pb", b2)])
                        s1, ks1 = tmp()
                        ACT(s1[:], PB[b2][:], AF.Tanh, [("pb", b2)], [ks1], scale=0.5)
                        STT(s1[:], s1[:], 1.0, PB[b1][:], ALU.add, ALU.mult, [ks1, ("pb", b1)], [ks1])
                        CP("pool", mergedT[:, m, :], s1[:], [ks1], [("mg", m)] + CKEYS)
                for qd in range(4):
                    slab_a, ka_ = next_slab("wag", 0, 8, qd * 512, 512)
                    for mm_ in range(2):
                        m = qd * 2 + mm_
                        b1, b2 = bank(), bank()
                        MM(PB[b1][:], [(slab_a[:, k, mm_ * 128:(mm_ + 1) * 128], OT[:, k, :]) for k in range(4)],
                           [ka_, "OT"], [("pb", b1)])
                        MM(PB[b2][:], [(slab_a[:, k, 256 + mm_ * 128:256 + (mm_ + 1) * 128], hT[:, k, ws]) for k in range(8)],
                           [ka_, hTk(w)], [("pb", b2)])
                        s2, ks2 = tmp()
                        ACT(s2[:], PB[b2][:], AF.Tanh, [("pb", b2)], [ks2], scale=0.5)
                        STT(s2[:], s2[:], 1.0, PB[b1][:], ALU.add, ALU.mult, [ks2, ("pb", b1)], [ks2])
                        TT("dve", mergedT[:, m, :], mergedT[:, m, :], s2[:], ALU.add, [ks2, ("mg", m)], [("mg", m)])
                slab_o = []
                for half in range(2):
                    slab_o.append(next_slab("wout", 0, 8, half * 512, 512))
                fbufF = fbuf[:].rearrange("p a b -> p (a b)")

                def e3_mm(tt):
                    tk = w * 4 + tt
                    tsl = slice(tt * 128, (tt + 1) * 128)
                    xb_ = xt[tt % 2]
                    kx = ("xt", tt % 2)
                    DMA("sp", xb_[:], x_d[b, tk * 128:(tk + 1) * 128, :], [], [kx])
                    bA, bB = bank_pair()
                    for half, bb in ((0, bA), (1, bB)):
                        so, kso = slab_o[half]
                        MM(PB[bb][:], [(mergedT[:, k, tsl], so[:, k, :]) for k in range(8)],
                           [kso] + [("mg", k) for k in range(8)], [("pb", bb)])
                    return bA, bB

                def e3_chain(tt, banks_):
                    tk = w * 4 + tt
                    tsl = slice(tt * 128, (tt + 1) * 128)
                    xb_ = xt[tt % 2]
                    kx = ("xt", tt % 2)
                    par = tt % 2
                    kpg = [("fbuf", 2 * par), ("fbuf", 2 * par + 1)]
                    for half, bb in enumerate(banks_):
                        hs = slice(half * 512, (half + 1) * 512)
                        ACT(xcb[:], PB[bb][:], AF.Square, [("pb", bb)], ["xcb", ("smx", half)], scale=0.5,
                            accum=sm[:, 2 + half:3 + half])
                        TT("dve", fbuf[:, 2 * par + half, :], PB[bb][:], gbc[:, 1, hs], ALU.mult, [("pb", bb), "gbc", ("smx", half)], [kpg[half]])
                    TT("dve", sm[:, 4:5], sm[:, 2:3], sm[:, 3:4], ALU.add, [("smx", 0), ("smx", 1)], ["sm4"])
                    rstd_from_ss(sm[:, 4:5], sm[:, 5:6], ["sm4"], "sm5", post_scale=0.5)
                    STT(x1t[:], fbufF[:, par * 1024:(par + 1) * 1024], sm[:, 5:6], xb_[:], ALU.mult, ALU.add,
                        kpg + ["sm5", kx], ["x1t"])
                    DMA("pool", x1_d[tk * 128:(tk + 1) * 128, :], x1t[:], ["x1t"], [("x1", tk)])
                    ACT(junk[:], x1t[:], AF.Square, ["x1t"], ["hb", "sm6"], accum=sm[:, 6:7])
                    rstd_from_ss(sm[:, 6:7], sm[:, 7:8], ["sm6"], "sm7")
                    STT(hb[:], x1t[:], sm[:, 7:8], gbc[:, 2, :], ALU.mult, ALU.mult, ["x1t", "sm7", "gbc"], ["hb"])
                    TR([(PT[:, k * 128:(k + 1) * 128], hb[:, k * 128:(k + 1) * 128]) for k in range(8)], ["hb"], ["PT"])
                    ACT(h2T[:, :, tsl], PT[:, :].rearrange("p (k t) -> p k t", k=8), AF.Copy, ["PT"],
                        ["h2T"] + CKEYS)

                pend3 = e3_mm(0)
                for tt in range(4):
                    nxt3 = e3_mm(tt + 1) if tt + 1 < 4 else None
                    e3_chain(tt, pend3)
                    pend3 = nxt3
                for pc in range(NFC // 2):
                    slab_gu, kgu = next_slab("wgu", 0, 8, pc * 512, 512)
                    for cc in range(2):
                        c = 2 * pc + cc
                        b1, b2 = bank(), bank()
                        MM(PB[b1][:], [(slab_gu[:, k, cc * 128:(cc + 1) * 128], h2T[:, k, :]) for k in range(8)],
                           [kgu, "h2T"], [("pb", b1)])
                        MM(PB[b2][:], [(slab_gu[:, k, 256 + cc * 128:256 + (cc + 1) * 128], h2T[:, k, :]) for k in range(8)],
                           [kgu, "h2T"], [("pb", b2)])
                        sg, ksg = tmp()
                        ACT(sg[:], PB[b1][:], AF.Tanh, [("pb", b1)], [ksg], scale=0.5)
                        STT(sg[:], sg[:], 1.0, PB[b1][:], ALU.add, ALU.mult, [ksg, ("pb", b1)], [ksg])
                        STT(aT[:, c, :], sg[:], 0.5, PB[b2][:], ALU.mult, ALU.mult, [ksg, ("pb", b2)],
                            [("aT", c)] + CKEYS)
                for nh in range(2):
                    hs = slice(nh * 512, (nh + 1) * 512)
                    b4 = [bank() for _ in range(4)]
                    for kh in range(2):
                        sK, kK = next_slab("wd", kh * 11, 11, nh * 512, 512)
                        for tt in range(4):
                            tsl = slice(tt * 128, (tt + 1) * 128)
                            MM(PB[b4[tt]][:], [(aT[:, kh * 11 + k, tsl], sK[:, k, :]) for k in range(11)],
                               [kK] + [("aT", kh * 11 + k) for k in range(11)], [("pb", b4[tt])],
                               first=(kh == 0), last=(kh == 1))
                    for tt in range(4):
                        tk = w * 4 + tt
                        bb = b4[tt]
                        col = 16 + tt * 2 + nh
                        ACT(junk[:, 0:512], PB[bb][:], AF.Square, [("pb", bb)], ["hb", ("smf", tt, nh)],
                            accum=sm[:, col:col + 1])
                        if nh == 0:
                            TT("dve", fbuf[:, tt, :], PB[bb][:], gbc[:, 3, 0:512], ALU.mult, [("pb", bb), "gbc", ("smf", tt, nh)], [("fbuf", tt)])
                        else:
                            ob_ = xt[tt % 2]
                            kx = ("xt", tt % 2)
                            TT("dve", ob_[:, 512:1024], PB[bb][:], gbc[:, 3, 512:1024], ALU.mult, [("pb", bb), "gbc", ("smf", tt, nh)], [kx])
                            TT("dve", sm[:, 4:5], sm[:, 16 + tt * 2:17 + tt * 2], sm[:, 17 + tt * 2:18 + tt * 2], ALU.add,
                               [("smf", tt, 0), ("smf", tt, 1)], ["sm4"])
                            rstd_from_ss(sm[:, 4:5], sm[:, 5:6], ["sm4"], "sm5")
                            DMA("sp", x1t[:], x1_d[tk * 128:(tk + 1) * 128, :], [("x1", tk)], ["x1t"])
                            ob_ = xt[tt % 2]
                            kx = ("xt", tt % 2)
                            STT(ob_[:, 0:512], fbuf[:, tt, :], sm[:, 5:6], x1t[:, 0:512], ALU.mult, ALU.add,
                                [("fbuf", tt), "sm5", "x1t"], [kx])
                            STT(ob_[:, 512:1024], ob_[:, 512:1024], sm[:, 5:6], x1t[:, 512:1024], ALU.mult, ALU.add,
                                [kx, "sm5", "x1t"], [kx])
                            DMA("pool", y_d[b, tk * 128:(tk + 1) * 128, :], ob_[:], [kx], [])
        S.finish()
        S.emit()
    return nc


def _tile_k(w):
    K, N = w.shape
    return np.ascontiguousarray(w.reshape(K // 128, 128, N).transpose(1, 0, 2))


def _pair_cols(a, b_):
    parts = []
    for qd in range(4):
        parts.append(a[:, qd * 256:(qd + 1) * 256])
        parts.append(b_[:, qd * 256:(qd + 1) * 256])
    return np.concatenate(parts, axis=1)


def _interleave_gu(wg, wu):
    parts = []
    for pc in range(NFC // 2):
        parts.append(wg[:, pc * 256:(pc + 1) * 256])
        parts.append(wu[:, pc * 256:(pc + 1) * 256])
    return np.concatenate(parts, axis=1)


def _prep_shared(pre_mix_norm, w_in, conv_w, conv_b, w_rg, b_rg, w_ig, b_ig, lru_lambda, w_lru_proj,
                 w_attn_proj, w_out, post_mix_norm, pre_ffn_norm, w_ffn_gate, w_ffn_up, w_ffn_down, post_ffn_norm):
    f32 = np.float32
    W = np.asarray(w_in[0], f32)
    cols = []
    for j in range(8):
        cols.append(np.arange(128 * j, 128 * j + 128))
        cols.append(np.arange(1024 + 128 * j, 1024 + 128 * j + 128))
    for g in range(3):
        q0, k0, v0 = 2048 + g * 512, 3584 + g * 512, 5120 + g * 512
        cols.append(np.arange(q0, q0 + 512))
        cols.append(np.arange(k0, k0 + 512))
        cols.append(np.arange(v0, v0 + 512))
        for base0 in (q0, k0):
            for h in range(4):
                cols.append(np.arange(base0 + h * 128 + 16, base0 + h * 128 + 32))
                cols.append(np.arange(base0 + h * 128, base0 + h * 128 + 16))
    cols.append(np.arange(6656, 8704))
    cols = np.concatenate(cols)
    assert cols.shape[0] == NCOLS
    w_in_t = _tile_k(W[:, cols])
    gains = np.stack([np.asarray(a[0], f32) for a in (pre_mix_norm, post_mix_norm, pre_ffn_norm, post_ffn_norm)])
    gains_bc = np.ascontiguousarray(np.broadcast_to(gains[:, None, :], (4, 128, D)))
    fm = lambda v: np.asarray(v, f32).reshape(8, 128).T
    fvec = np.concatenate([fm(conv_w[0][j]) for j in range(4)] + [fm(conv_b[0]), fm(b_rg[0]), fm(b_ig[0]), fm(lru_lambda[0])], axis=1)
    fvec = np.ascontiguousarray(fvec, f32)

    def blk(wb):
        wb = np.asarray(wb[0], f32)
        out = np.zeros((128, 8, 128), f32)
        for j in range(8):
            out[0:64, j, 0:64] = wb[2 * j]
            out[64:128, j, 64:128] = wb[2 * j + 1]
        return out
    inv_freq = (np.float32(500000.0) ** (-(np.arange(0, 32, 2, dtype=f32)) / np.float32(32))).astype(f32)
    p = np.arange(128)
    rope_c = np.stack([inv_freq[p % 16], np.where((p % 32) < 16, -2.0, 2.0).astype(f32)], axis=1).astype(f32)
    return {
        "gains_bc": gains_bc, "w_in_t": w_in_t, "fvec": fvec, "rope_c": np.ascontiguousarray(rope_c),
        "wrg_blk": blk(w_rg), "wig_blk": blk(w_ig),
        "w_lru_t": _tile_k(np.asarray(w_lru_proj[0], f32)), "w_attn_t": _tile_k(np.asarray(w_attn_proj[0], f32)),
        "w_out_t": _tile_k(np.asarray(w_out[0], f32)),
        "w_lg_t": _tile_k(_pair_cols(np.asarray(w_lru_proj[0], f32), W[:, 6656:7680])),
        "w_ag_t": _tile_k(_pair_cols(np.concatenate([np.asarray(w_attn_proj[0], f32), np.zeros((512, D), f32)], axis=0), W[:, 7680:8704])), "w_gu_t": _tile_k(_interleave_gu(np.asarray(w_ffn_gate[0], f32), np.asarray(w_ffn_up[0], f32))),
        "w_down_t": _tile_k(np.asarray(w_ffn_down[0], f32)),
    }


def make_in_maps(x, positions, shared, n_cores):
    B = x.shape[0]
    per = B // n_cores
    maps = []
    for c in range(n_cores):
        pos = np.asarray(positions[c * per:(c + 1) * per], np.int32)
        m = dict(shared)
        m["x"] = np.ascontiguousarray(np.asarray(x[c * per:(c + 1) * per], np.float32))
        m["pos_bc"] = np.ascontiguousarray(np.broadcast_to(pos[:, None, :], (per, 128, SEQ)))
        maps.append(m)
    return maps


def kernel(x, positions, pre_mix_norm, w_in, conv_w, conv_b, w_rg, b_rg, w_ig, b_ig, lru_lambda, w_lru_proj,
           w_attn_proj, w_out, post_mix_norm, pre_ffn_norm, w_ffn_gate, w_ffn_up, w_ffn_down, post_ffn_norm):
    x = np.asarray(x)
    shared = _prep_shared(pre_mix_norm, w_in, conv_w, conv_b, w_rg, b_rg, w_ig, b_ig, lru_lambda, w_lru_proj,
                          w_attn_proj, w_out, post_mix_norm, pre_ffn_norm, w_ffn_gate, w_ffn_up, w_ffn_down, post_ffn_norm)
    maps = make_in_maps(x, np.asarray(positions), shared, NCORES)
    nc = build(x.shape[0] // NCORES)
    res = run_bass_kernel_spmd(nc, maps, core_ids=list(range(NCORES)))
    return np.concatenate([np.asarray(r["y"], np.float32) for r in res.results], axis=0)
```
